# Optimizing a Trainium2 kernel written in Bass

```python
import math
import jax, jax.numpy as jnp
from jax import lax
import numpy as np

D_MODEL = 1024
BATCH = 16
SEQ = 2048
DEPTH = 1
DEC_BATCH = 128
DEC_SEQ = 8
PAST_LEN = 16384
PAGE_SIZE = 128

D_MIX = D_MODEL
D_ATT = D_MIX // 2
D_RET = D_MIX - D_ATT
HEAD_DIM_A = 64
N_HEADS_A = D_ATT // HEAD_DIM_A
N_KV_HEADS_A = 2
GROUP_A = N_HEADS_A // N_KV_HEADS_A
KV_W = N_KV_HEADS_A * HEAD_DIM_A
WINDOW = 128
BLOCK_A = WINDOW
N_HEADS_R = 4
HEAD_DIM_R = D_RET // N_HEADS_R
CHUNK_R = 128
ROPE_BASE = 10000.0
LN_EPS = 1e-5
GN_EPS = 1e-5
DEEPNORM_ALPHA = float((2 * DEPTH) ** 0.25)
DEEPNORM_BETA = float((8 * DEPTH) ** -0.25)
SPLIT_SIZES = (D_ATT, KV_W, KV_W, D_ATT, D_RET, D_RET, D_RET, D_RET)
D_IN = sum(SPLIT_SIZES)
SPLIT_POINTS = [int(s) for s in np.cumsum(SPLIT_SIZES)[:-1]]

kernel_name = "hymba_swa_sink_retention_deepnorm_adaln_step"


def _front(x, c, w_ada, b_ada, w_in):
    B, T, _ = x.shape
    cond = jax.nn.silu(c) @ w_ada + b_ada
    shift, scale, gate = jnp.split(cond, 3, axis=-1)
    h = x * (1.0 + scale[:, None, :]) + shift[:, None, :]
    z = h @ w_in
    q_a, k_a, v_a, g_a, q_r, k_r, v_r, g_r = jnp.split(z, SPLIT_POINTS, axis=-1)
    q_a = q_a.reshape(B, T, N_KV_HEADS_A, GROUP_A, HEAD_DIM_A)
    k_a = k_a.reshape(B, T, N_KV_HEADS_A, HEAD_DIM_A)
    v_a = v_a.reshape(B, T, N_KV_HEADS_A, HEAD_DIM_A)
    q_r = q_r.reshape(B, T, N_HEADS_R, HEAD_DIM_R)
    k_r = k_r.reshape(B, T, N_HEADS_R, HEAD_DIM_R)
    v_r = v_r.reshape(B, T, N_HEADS_R, HEAD_DIM_R)
    return (q_a, k_a, v_a, g_a, q_r, k_r, v_r, g_r), gate


def _back(x, gate, o_a, g_a, o_r, g_r, gn_w, w_out, ln_w, ln_b):
    B, T, _ = x.shape
    o_r = o_r.astype(jnp.float32)
    mu = o_r.mean(-1, keepdims=True)
    var = jnp.square(o_r - mu).mean(-1, keepdims=True)
    o_r = ((o_r - mu) * lax.rsqrt(var + GN_EPS)).reshape(B, T, D_RET).astype(x.dtype) * gn_w
    o_a = o_a.reshape(B, T, D_ATT).astype(x.dtype)
    mixed = jnp.concatenate([o_a * jax.nn.silu(g_a), o_r * jax.nn.silu(g_r)], axis=-1)
    y = mixed @ w_out
    r = (DEEPNORM_ALPHA * x + gate[:, None, :] * y).astype(jnp.float32)
    mu = r.mean(-1, keepdims=True)
    var = jnp.square(r - mu).mean(-1, keepdims=True)
    rn = (r - mu) * lax.rsqrt(var + LN_EPS)
    return (rn * ln_w + ln_b).astype(x.dtype)


def _rotary(x, pos):
    half = x.shape[-1] // 2
    inv = ROPE_BASE ** (-jnp.arange(half, dtype=jnp.float32) / half)
    ang = pos[:, None] * inv[None, :]
    cos = jnp.cos(ang)[:, None, :]
    sin = jnp.sin(ang)[:, None, :]
    xf = x.astype(jnp.float32)
    x1, x2 = xf[..., :half], xf[..., half:]
    return jnp.concatenate([x1 * cos - x2 * sin, x2 * cos + x1 * sin], axis=-1).astype(x.dtype)


def _sink_attention(q, k, v, qpos, kpos, sinks):
    s = jnp.einsum('bnqhgd,bnkhd->bnhgqk', q, k).astype(jnp.float32) * (HEAD_DIM_A ** -0.5)
    rel = qpos[:, :, None] - kpos[:, None, :]
    valid = (rel >= 0) & (rel <= WINDOW) & (kpos[:, None, :] >= 0)
    s = jnp.where(valid[None, :, None, None], s, -1e30)
    sink = sinks.astype(jnp.float32).reshape(1, 1, N_KV_HEADS_A, GROUP_A, 1, 1)
    m = jnp.maximum(s.max(-1, keepdims=True), sink)
    p = jnp.exp(s - m)
    p = p / (p.sum(-1, keepdims=True) + jnp.exp(sink - m))
    return jnp.einsum('bnhgqk,bnkhd->bnqhgd', p.astype(v.dtype), v)


def _swa_prompt(q, k, v, sinks):
    B, T = q.shape[0], q.shape[1]
    nb = T // BLOCK_A
    qb = q.reshape(B, nb, BLOCK_A, N_KV_HEADS_A, GROUP_A, HEAD_DIM_A)
    kb = k.reshape(B, nb, BLOCK_A, N_KV_HEADS_A, HEAD_DIM_A)
    vb = v.reshape(B, nb, BLOCK_A, N_KV_HEADS_A, HEAD_DIM_A)
    kk = jnp.concatenate([jnp.concatenate([jnp.zeros_like(kb[:, :1]), kb[:, :-1]], axis=1), kb], axis=2)
    vv = jnp.concatenate([jnp.concatenate([jnp.zeros_like(vb[:, :1]), vb[:, :-1]], axis=1), vb], axis=2)
    start = jnp.arange(nb)[:, None] * BLOCK_A
    qpos = start + jnp.arange(BLOCK_A)[None, :]
    kpos = start - BLOCK_A + jnp.arange(2 * BLOCK_A)[None, :]
    o = _sink_attention(qb, kk, vv, qpos, kpos, sinks)
    return o.reshape(B, T, N_KV_HEADS_A, GROUP_A, HEAD_DIM_A)


def _swa_sample(q, k, v, cache_k, cache_v, sinks):
    keys = jnp.concatenate([cache_k.astype(k.dtype), k], axis=1)
    vals = jnp.concatenate([cache_v.astype(v.dtype), v], axis=1)
    T = q.shape[1]
    qpos = (PAST_LEN + jnp.arange(T))[None, :]
    kpos = (PAST_LEN - WINDOW + jnp.arange(WINDOW + T))[None, :]
    o = _sink_attention(q[:, None], keys[:, None], vals[:, None], qpos, kpos, sinks)[:, 0]
    return o, keys[:, -WINDOW:], vals[:, -WINDOW:]


def _retention_chunk(q, k, v, S, log_gamma):
    L = q.shape[1]
    idx = jnp.arange(L, dtype=jnp.float32)
    diff = idx[:, None] - idx[None, :]
    dmat = jnp.where(diff >= 0, jnp.exp(log_gamma[:, None, None] * jnp.maximum(diff, 0.0)), 0.0)
    qf, kf, vf, Sf = (a.astype(jnp.float32) for a in (q, k, v, S))
    s = jnp.einsum('bihd,bjhd->bhij', qf, kf) * dmat[None]
    intra = jnp.einsum('bhij,bjhe->bihe', s, vf)
    q_decay = jnp.exp(log_gamma[None, :] * (idx[:, None] + 1.0))
    inter = jnp.einsum('bihd,bhde->bihe', qf, Sf) * q_decay[None, :, :, None]
    k_decay = jnp.exp(log_gamma[None, :] * (L - 1.0 - idx[:, None]))
    S_new = jnp.exp(log_gamma * L)[None, :, None, None] * Sf + jnp.einsum(
        'bjhd,bjhe->bhde', kf * k_decay[None, :, :, None], vf)
    return intra + inter, S_new


def _retention_prompt(q, k, v, log_gamma):
    B, T = q.shape[0], q.shape[1]
    nc = T // CHUNK_R

    def to_chunks(a):
        return jnp.moveaxis(a.reshape(B, nc, CHUNK_R, N_HEADS_R, HEAD_DIM_R), 1, 0)

    def step(S, qkv):
        qc, kc, vc = qkv
        o, S = _retention_chunk(qc, kc, vc, S, log_gamma)
        return S, o

    S0 = jnp.zeros((B, N_HEADS_R, HEAD_DIM_R, HEAD_DIM_R), jnp.float32)
    S, o = lax.scan(step, S0, (to_chunks(q), to_chunks(k), to_chunks(v)))
    o = jnp.moveaxis(o, 0, 1).reshape(B, T, N_HEADS_R, HEAD_DIM_R)
    return o, S


def setup_inputs(seed: int = 0) -> dict:
    key = jax.random.key(seed)
    ks = jax.random.split(key, 16)
    nrm = jax.random.normal
    f32 = jnp.float32
    return {
        "x_prompt": nrm(ks[0], (BATCH, SEQ, D_MODEL), f32),
        "x_sample": nrm(ks[1], (DEC_BATCH, DEC_SEQ, D_MODEL), f32),
        "c_prompt": nrm(ks[2], (BATCH, D_MODEL), f32),
        "c_sample": nrm(ks[3], (DEC_BATCH, D_MODEL), f32),
        "cache_k_win": nrm(ks[4], (DEPTH, DEC_BATCH, WINDOW, N_KV_HEADS_A, HEAD_DIM_A), f32),
        "cache_v_win": nrm(ks[5], (DEPTH, DEC_BATCH, WINDOW, N_KV_HEADS_A, HEAD_DIM_A), f32),
        "state_ret": 0.5 * nrm(ks[6], (DEPTH, DEC_BATCH, N_HEADS_R, HEAD_DIM_R, HEAD_DIM_R), f32),
        "w_ada": 0.5 * D_MODEL ** -0.5 * nrm(ks[7], (DEPTH, D_MODEL, 3 * D_MODEL), f32),
        "b_ada": 0.02 * nrm(ks[8], (DEPTH, 3 * D_MODEL), f32),
        "w_in": D_MODEL ** -0.5 * nrm(ks[9], (DEPTH, D_MODEL, D_IN), f32),
        "attn_sinks": 0.5 * nrm(ks[10], (DEPTH, N_HEADS_A), f32),
        "ret_gn_w": 1.0 + 0.02 * nrm(ks[11], (DEPTH, D_RET), f32),
        "w_out": DEEPNORM_BETA * D_MIX ** -0.5 * nrm(ks[12], (DEPTH, D_MIX, D_MODEL), f32),
        "ln_w": 1.0 + 0.02 * nrm(ks[13], (DEPTH, D_MODEL), f32),
        "ln_b": 0.02 * nrm(ks[14], (DEPTH, D_MODEL), f32),
    }


def reference(x_prompt, x_sample, c_prompt, c_sample, cache_k_win, cache_v_win, state_ret,
              w_ada, b_ada, w_in, attn_sinks, ret_gn_w, w_out, ln_w, ln_b):
    log_gamma = jnp.log(1.0 - 2.0 ** (-5.0 - jnp.arange(N_HEADS_R, dtype=jnp.float32)))
    pos_p = jnp.arange(x_prompt.shape[1], dtype=jnp.float32)
    pos_s = (PAST_LEN + jnp.arange(x_sample.shape[1])).astype(jnp.float32)
    x_p, x_s = x_prompt, x_sample
    kp_l, vp_l, sp_l, ks_l, vs_l, ss_l = [], [], [], [], [], []
    for l in range(DEPTH):
        (q_a, k_a, v_a, g_a, q_r, k_r, v_r, g_r), gate = _front(x_p, c_prompt, w_ada[l], b_ada[l], w_in[l])
        o_a = _swa_prompt(q_a, k_a, v_a, attn_sinks[l])
        q_r = _rotary(q_r, pos_p)
        k_r = _rotary(k_r, pos_p) * (HEAD_DIM_R ** -0.5)
        o_r, S_p = _retention_prompt(q_r, k_r, v_r, log_gamma)
        kp_l.append(k_a[:, -WINDOW:])
        vp_l.append(v_a[:, -WINDOW:])
        sp_l.append(S_p)
        x_p = _back(x_p, gate, o_a, g_a, o_r, g_r, ret_gn_w[l], w_out[l], ln_w[l], ln_b[l])
        (q_a, k_a, v_a, g_a, q_r, k_r, v_r, g_r), gate = _front(x_s, c_sample, w_ada[l], b_ada[l], w_in[l])
        o_a, k_buf, v_buf = _swa_sample(q_a, k_a, v_a, cache_k_win[l], cache_v_win[l], attn_sinks[l])
        q_r = _rotary(q_r, pos_s)
        k_r = _rotary(k_r, pos_s) * (HEAD_DIM_R ** -0.5)
        o_r, S_s = _retention_chunk(q_r, k_r, v_r, state_ret[l], log_gamma)
        ks_l.append(k_buf)
        vs_l.append(v_buf)
        ss_l.append(S_s)
        x_s = _back(x_s, gate, o_a, g_a, o_r, g_r, ret_gn_w[l], w_out[l], ln_w[l], ln_b[l])
    return (x_p, x_s, jnp.stack(kp_l), jnp.stack(vp_l), jnp.stack(sp_l),
            jnp.stack(ks_l), jnp.stack(vs_l), jnp.stack(ss_l))
```

```python
import math
from contextlib import ExitStack

import numpy as np
import ml_dtypes
import concourse.bass as bass
import concourse.mybir as mybir
from concourse.bass_utils import run_bass_kernel_spmd

F32 = mybir.dt.float32
BF16 = mybir.dt.bfloat16
AF = mybir.ActivationFunctionType
ALU = mybir.AluOpType

P = 128
D = 1024
DIN = 3328
NT = 16
NB = 2
NS = 16
NCORES = 8
PAST = 16384
C_QA, C_K, C_V, C_GA, C_QR, C_KR, C_VR, C_GR = 0, 512, 640, 768, 1280, 1792, 2304, 2816
LN_EPS = 1e-5
GN_EPS = 1e-5
ALPHA = float(2.0 ** 0.25)
GAM = [1.0 - 2.0 ** (-5.0 - h) for h in range(4)]
ENGS = ["pe", "act", "dve", "pool", "sp"]
BIG = 1 << 30


TAGS = None


class Sched:
    def __init__(self):
        self.ops = {e: [] for e in ENGS}
        self.cnt = {}
        self.acc = {}
        self.waited = {e: {} for e in ENGS}
        self.cur = None

    @staticmethod
    def _norm(a):
        if isinstance(a, str):
            return (a, 0, BIG)
        return a

    def add(self, eng, fn, reads=(), writes=(), dma=None):
        pid = eng if dma is None else "D:" + dma
        inc = 1 if dma is None else 16
        deps = {}

        def need(pv):
            if pv is None:
                return
            p, v = pv
            if p == "pe" and eng == "pe" and dma is None:
                return
            if deps.get(p, 0) < v:
                deps[p] = v

        reads = [self._norm(a) for a in reads]
        writes = [self._norm(a) for a in writes]
        for (nm, lo, hi) in reads:
            for (l2, h2), ent in self.acc.get(nm, {}).items():
                if l2 < hi and lo < h2:
                    need(ent["w"])
        for (nm, lo, hi) in writes:
            for (l2, h2), ent in self.acc.get(nm, {}).items():
                if l2 < hi and lo < h2:
                    need(ent["w"])
                    for p, v in ent["r"].items():
                        need((p, v))
        waits = []
        for p, v in deps.items():
            if self.waited[eng].get(p, 0) >= v:
                continue
            self.waited[eng][p] = v
            waits.append((p, v))
        val = self.cnt.get(pid, 0) + inc
        self.cnt[pid] = val
        tag = ""
        if TAGS is not None:
            import sys as _sys
            f = _sys._getframe(1)
            names = []
            while f is not None and f.f_code.co_name != "build_program":
                names.append(f.f_code.co_name)
                f = f.f_back
            tag = "/".join(reversed(names[1:])) + (":" + str(self.cur) if self.cur is not None else "")
        self.ops[eng].append((fn, waits, pid, inc, tag))
        for (nm, lo, hi) in reads:
            ent = self.acc.setdefault(nm, {}).setdefault((lo, hi), {"w": None, "r": {}})
            ent["r"][pid] = val
        for (nm, lo, hi) in writes:
            ent = self.acc.setdefault(nm, {}).setdefault((lo, hi), {"w": None, "r": {}})
            ent["w"] = (pid, val)
            ent["r"] = {}


def _bf(a):
    return np.asarray(a, dtype=np.float32).astype(ml_dtypes.bfloat16)


def make_consts():
    c = {}
    c["identb"] = _bf(np.eye(128))
    c["identf"] = np.eye(128, dtype=np.float32)
    half = 64
    inv = (np.float32(10000.0) ** (-np.arange(half, dtype=np.float32) / np.float32(half))).astype(np.float32)

    def tab(pos):
        ang = (pos.astype(np.float32)[:, None] * inv[None, :]).astype(np.float32)
        cs = np.cos(ang).astype(np.float32)
        sn = np.sin(ang).astype(np.float32)
        return np.concatenate([cs, cs, -sn, sn], axis=1).astype(np.float32)

    pos_p = np.arange(NT * 128)
    c["tabp"] = tab(pos_p).reshape(NT, 128, 256)
    pos_s = PAST + (np.arange(128) % 8)
    tabs = tab(pos_s)
    g = np.array(GAM, dtype=np.float64)
    rs = 1.0 / math.sqrt(128.0)
    j = np.arange(128)
    tri = (j[None, :] >= j[:, None]).astype(np.float64)
    mR = tri[:, None, :] * (g[None, :, None] ** (-(j[:, None, None] + 1.0))) * rs
    c["maskR"] = mR.reshape(128, 512).astype(np.float32)
    sj, jj = j // 8, j % 8
    same = (sj[:, None] == sj[None, :])
    tri_s = same & (jj[None, :] >= jj[:, None])
    mRs = tri_s[:, None, :] * (g[None, :, None] ** (-(jj[:, None, None] + 1.0))) * rs
    maskRs = mRs.reshape(128, 512).astype(np.float32)
    sm = np.zeros((128, 36), dtype=np.float64)
    sm[:, 0:4] = GN_EPS / (g[None, :] ** (2.0 * (j[:, None] + 1.0)))
    sm[:, 4:8] = (g[None, :] ** (127.0 - j[:, None])) * rs
    sm[:, 8:12] = GN_EPS / (g[None, :] ** (2.0 * (jj[:, None] + 1.0)))
    sm[:, 12:16] = (g[None, :] ** (7.0 - jj[:, None])) * rs
    sm[:, 16:20] = -0.5
    sm[:, 20:36] = (sj[:, None] == np.arange(16)[None, :])
    c["smalls"] = sm.astype(np.float32)
    mprev = (j[:, None] >= j[None, :])
    mcur = (j[:, None] <= j[None, :])
    NEG = -30000.0
    neg = lambda m: np.where(m, 0.0, NEG)
    c["cbp"] = _bf(np.stack([neg(mprev), neg(mcur)], axis=1).reshape(128, 256))
    Mn = same & (jj[:, None] <= jj[None, :])
    Mc = (np.arange(16)[None, :, None] == sj[None, None, :]) & (j[:, None, None] >= jj[None, None, :])
    selT = np.broadcast_to((np.arange(16)[None, :, None] == sj[None, None, :]), (128, 16, 128))
    c["cbs"] = _bf(np.concatenate([neg(Mn).reshape(128, 128), neg(Mc).reshape(128, 2048), selT.reshape(128, 2048)], axis=1))
    c["cfs"] = np.concatenate([maskRs, tabs], axis=1).astype(np.float32)
    return c


def build_program(taps=None, ntiles=None, do_sample=True, phase0_only=False):
    nc = bass.Bass("TRN2", target_bir_lowering=False)
    S = Sched()
    es = ExitStack()

    def din(name, shape, dt=F32):
        return nc.dram_tensor(name, list(shape), dt, kind="ExternalInput").ap()

    def dout(name, shape, dt=F32):
        return nc.dram_tensor(name, list(shape), dt, kind="ExternalOutput").ap()

    xp = din("xp", [NB, NT * 128, D])
    xs = din("xs", [128, D])
    c18 = din("c18", [18, D])
    ck = din("ck", [NS, 128, 128])
    cv = din("cv", [NS, 128, 128])
    sr = din("sr", [NS, 4, 128, 128])
    wada = din("wada", [D, 3 * D])
    bada = din("bada", [24, 128])
    win = din("win", [D, DIN])
    sinks = din("sinks", [8])
    gnw = din("gnw", [4, 128])
    wout = din("wout", [D, D])
    lnw = din("lnw", [D])
    lnb = din("lnb", [D])
    d_identb = din("identb", [128, 128], BF16)
    d_identf = din("identf", [128, 128])
    d_tabp = din("tabp", [NT, 128, 256])
    d_maskR = din("maskR", [128, 512])
    d_smalls = din("smalls", [128, 36])
    d_cbp = din("cbp", [128, 256], BF16)
    d_cbs = din("cbs", [128, 4224], BF16)
    d_cfs = din("cfs", [128, 768])

    yp = dout("yp", [NB, NT * 128, D])
    ys = dout("ys", [128, D])
    kwp = dout("kwp", [NB, 128, 128])
    vwp = dout("vwp", [NB, 128, 128])
    srp = dout("srp", [NB, 4, 128, 128])
    kws = dout("kws", [NS, 128, 128])
    vws = dout("vws", [NS, 128, 128])
    srs = dout("srs", [NS, 4, 128, 128])

    def sb(name, shape, dt):
        return es.enter_context(nc.sbuf_tensor(name, list(shape), dt))

    win_bf = sb("win_bf", [P, 8, DIN], BF16)
    wout_bf = sb("wout_bf", [P, 8, D], BF16)
    identb = sb("identb_s", [P, 128], BF16)
    identf = sb("identf_s", [P, 128], F32)
    smalls = sb("smalls_s", [P, 36], F32)
    cbp = sb("cbp_s", [P, 2, 128], BF16)
    maskR = sb("maskR_s", [P, 4, 128], F32)
    lnw_bc = sb("lnw_bc", [P, D], F32)
    lnb_bc = sb("lnb_bc", [P, D], F32)
    esink = sb("esink", [P, 8], F32)
    condT = sb("condT", [P, 24, 18], F32)
    scT = sb("scT", [P, 8, 18], BF16)
    bcol = sb("bcol", [P, 24], F32)
    gnwh = sb("gnwh", [P, 512], BF16)
    gate_bc = sb("gate_bc", [P, 3, D], F32)
    gbt = sb("gbt", [P, 2, 128], F32)
    xt = sb("xt", [P, 3, D], F32)
    xb = sb("xb", [P, D], BF16)
    hT = sb("hT", [P, 2, 8, 128], BF16)
    tabt = sb("tabt", [P, 2, 256], F32)
    qa_b = sb("qa_b", [P, 512], BF16)
    kv_f = sb("kv_f", [P, 256], F32)
    ka_b = sb("ka_b", [P, 128], BF16)
    va1 = sb("va1", [P, 2, 2, 65], BF16)
    tg = sb("tg", [P, 512], F32)
    tg2 = sb("tg2", [P, 512], F32)
    sg_a = sb("sg_a", [P, 512], BF16)
    sg_r = sb("sg_r", [P, 512], BF16)
    rA = sb("rA", [P, 512], F32)
    rB = sb("rB", [P, 512], F32)
    rA2 = sb("rA2", [P, 512], F32)
    rB2 = sb("rB2", [P, 512], F32)
    qr_b = sb("qr_b", [P, 4, 128], BF16)
    kr_b = sb("kr_b", [P, 4, 128], BF16)
    ks_b = sb("ks_b", [P, 4, 128], BF16)
    vr_b = sb("vr_b", [P, 4, 128], BF16)
    qaT = sb("qaT", [P, 8, 128], BF16)
    kaT = sb("kaT", [P, 2, 2, 128], BF16)
    qkrT = sb("qkrT", [P, 8, 128], BF16)
    Eb = sb("Eb", [P, 4, 512], BF16)
    Pr = sb("Pr", [P, 4, 128], BF16)
    yn = sb("yn", [P, 4, 128], BF16)
    mixed = sb("mixed", [P, D], BF16)
    mixT = sb("mixT", [P, 8, 128], BF16)
    Sst = sb("Sst", [P, 4, 128], F32)
    Sbf = sb("Sbf", [P, 4, 128], BF16)
    ot = sb("ot", [P, 2, D], F32)
    scr = sb("scr", [P, 64], F32)
    arF = sb("arF", [P, 2816], F32)
    arB = sb("arB", [P, 15008], BF16)
    ps = es.enter_context(nc.psum_tensor("ps", [P, 4096], F32))

    def bank(k, lo=0, hi=512):
        return ps[:, k * 512 + lo:k * 512 + hi]

    def bankb(k):
        return ps[:, k * 512:(k + 1) * 512].bitcast(BF16)

    def PSK(k, lo=0, hi=512):
        return ("ps", k * 512 + lo, k * 512 + hi)

    den = scr[:, 0:8]
    rden = scr[:, 8:16]
    st_gn = scr[:, 16:40]
    mv_gn = scr[:, 40:48]
    vpe = scr[:, 48:52]
    rstd = scr[:, 52:56]
    nbg = scr[:, 56:60]
    scr2 = sb("scr2", [P, 32], F32)
    st_ln = scr2[:, 0:12]
    mv_ln = scr2[:, 12:14]
    vpl = scr2[:, 14:15]
    rsl = scr2[:, 15:16]
    nbl = scr2[:, 16:17]
    esk2 = sb("esk2", [P, 8], F32)

    def DMA(out, in_, reads, writes, sem, q="sp"):
        S.add(q, lambda e: e.dma_start(out=out, in_=in_), reads, writes, dma=sem)

    def MM(out, lhsT, rhs, start, stop, reads, writes):
        S.add("pe", lambda e: e.matmul(out, lhsT, rhs, start=start, stop=stop), reads, writes)

    def TR(out, in_, ident, reads, writes):
        S.add("pe", lambda e: e.transpose(out, in_, ident), reads, writes)

    def ACT(out, in_, func, reads, writes, bias=None, scale=None):
        kw = {}
        if bias is not None:
            kw["bias"] = bias
        if scale is not None:
            kw["scale"] = scale
        S.add("act", lambda e: e.activation(out=out, in_=in_, func=func, **kw), reads, writes)

    def TT(eng, out, in0, in1, op, reads, writes):
        S.add(eng, lambda e: e.tensor_tensor(out, in0, in1, op), reads, writes)

    def TS(eng, out, in0, s1, s2, op0, op1, reads, writes):
        if s2 is None:
            S.add(eng, lambda e: e.tensor_scalar(out, in0, s1, None, op0), reads, writes)
        else:
            S.add(eng, lambda e: e.tensor_scalar(out, in0, s1, s2, op0, op1), reads, writes)

    def STT(out, in0, sc, in1, op0, op1, reads, writes):
        S.add("dve", lambda e: e.scalar_tensor_tensor(out, in0, sc, in1, op0, op1), reads, writes)

    def CP(eng, out, in_, reads, writes):
        if eng == "act":
            S.add("act", lambda e: e.copy(out, in_), reads, writes)
        else:
            S.add(eng, lambda e: e.tensor_copy(out, in_), reads, writes)

    def MEMSET(eng, ap, val, writes):
        S.add(eng, lambda e: e.memset(ap, val), (), writes)

    tapl = []

    def TAP(name, ap, shape, key, dt=F32):
        if taps is None or name not in taps:
            return
        d = dout("tap_" + name, shape, dt)
        DMA(d, ap, [key], [], "tap_" + name)
        tapl.append(name)

    DMA(identb[:], d_identb, [], ["identb"], "c0")
    DMA(identf[:], d_identf, [], ["identf"], "c0")
    DMA(smalls[:], d_smalls, [], ["smalls"], "c0")
    DMA(xt[0:18, 1, :], c18, [], [("xt", 1024, 2048)], "c0")
    DMA(xt[32:56, 1, 0:128], bada, [], [("xtb", 0, 1)], "c0")
    DMA(tg2[:], gnw.rearrange("a b -> (a b)").partition_broadcast(128), [], ["tg2"], "c0")
    DMA(esink[:], sinks.partition_broadcast(128), [], ["esink"], "c0")
    DMA(xt[:, 0, :], xs, [], [("xt", 0, 1024)], "x0")

    kTc_all = arB[:, 4096:8192].rearrange("p (s h c) -> p s h c", s=16, h=2)
    kTcK = ("arB", 4096, 8192)
    va1c_all = arB[:, 8192:10272].rearrange("p (s h d) -> p s h d", s=16, h=2)
    va1cK = ("arB", 8192, 10272)
    ckb = [arB[:, 14496 + s * 256:14496 + (s + 1) * 256].rearrange("p (s d) -> p s d", s=2) for s in range(2)]
    ckbK = [("arB", 14496 + s * 256, 14496 + (s + 1) * 256) for s in range(2)]
    cstg = [xt[:, 2, s * 512:(s + 1) * 512].rearrange("p (kv s d) -> p kv s d", kv=2, s=2) for s in range(2)]
    MEMSET("pool", va1[:, :, :, 64:65], 2.0, [("va1", 0, 2)])
    if do_sample:
        MEMSET("pool", va1c_all[:, :, :, 64:65], 2.0, [va1cK])
        for grp in range(8):
            sl = grp % 2
            kK = ("xt", 2048 + sl * 512, 2048 + sl * 512 + 256)
            vK = ("xt", 2048 + sl * 512 + 256, 2048 + sl * 512 + 512)
            DMA(cstg[sl][:, 0, :, :], ck[2 * grp:2 * grp + 2].rearrange("s c d -> c s d"), [], [kK], "ck%d" % sl)
            DMA(cstg[sl][:, 1, :, :], cv[2 * grp:2 * grp + 2].rearrange("s c d -> c s d"), [], [vK], "cv%d" % sl)
            CP("dve", ckb[sl], cstg[sl][:, 0, :, :], [kK], [ckbK[sl]])
            CP("act", va1c_all[:, 2 * grp:2 * grp + 2, :, 0:64],
               cstg[sl][:, 1, :, :].rearrange("p s (h d) -> p s h d", h=2), [vK], [va1cK])
            t1 = bankb(1)
            for s2 in range(2):
                for h in range(2):
                    q = s2 * 2 + h
                    TR(t1[0:64, q * 128:(q + 1) * 128], ckb[sl][:, s2, h * 64:(h + 1) * 64], identb[:],
                       [ckbK[sl], "identb"], [PSK(1)])
            CP("dve", kTc_all[0:64, 2 * grp:2 * grp + 2, :, :].rearrange("p s h c -> p (s h c)"), t1[0:64, 0:512],
               [PSK(1)], [kTcK])
        DMA(arB[:, 10272:14496], d_cbs, [], [("arB", 10272, 14496)], "c1")
        DMA(tabt[:, 0, :], d_cfs[:, 512:768], [], [("tabt", 0, 1)], "c1")
    DMA(cbp[:].rearrange("p a b -> p (a b)"), d_cbp, [], ["cbp"], "c1")
    DMA(maskR[:].rearrange("p a b -> p (a b)"), d_maskR, [], ["maskR"], "c1")
    DMA(lnw_bc[:], lnw.partition_broadcast(128), [], ["lnw_bc"], "c1")
    DMA(lnb_bc[:], lnb.partition_broadcast(128), [], ["lnb_bc"], "c1")

    TS("dve", gnwh[:], tg2[:], 0.5, None, ALU.mult, None, ["tg2"], ["gnwh"])
    ACT(esink[:], esink[:], AF.Exp, ["esink"], ["esink"], bias=float(math.log(2.0)))
    for hf, tb_ in enumerate((tg, tg2)):
        cs = xt[0:18, 1, hf * 512:(hf + 1) * 512]
        ACT(tb_[0:18, :], cs, AF.Tanh, [("xt", 1024, 2048)], ["tg" if hf == 0 else "tg2"], scale=0.5)
        STT(cs, tb_[0:18, :], 1.0, cs, ALU.add, ALU.mult, ["tg" if hf == 0 else "tg2", ("xt", 1024, 2048)],
            [("xt", 1024, 2048)])
    for kc in range(8):
        TR(bank(0, kc * 18, kc * 18 + 18), xt[0:18, 1, kc * 128:(kc + 1) * 128], identf[0:18, 0:18],
           [("xt", 1024, 2048), "identf"], [PSK(0)])
    ACT(scT[:].rearrange("p a b -> p (a b)"), bank(0, 0, 144), AF.Identity, [PSK(0)], ["scT"], scale=0.5)
    TR(bank(0, 160, 184), xt[32:56, 1, 0:128], identf[32:56, 32:56], [("xtb", 0, 1), "identf"], [PSK(0)])
    CP("dve", bcol[:], bank(0, 160, 184), [PSK(0)], ["bcol"])

    wada_v = wada.rearrange("(kc p) n -> p kc n", p=128)
    win_v = win.rearrange("(kc p) n -> p kc n", p=128)
    wout_v = wout.rearrange("(kc p) n -> p kc n", p=128)
    grp_cols = [(C_QR, 512), (C_KR, 512), (C_QA, 512), (C_K, 256), (C_VR, 512), (C_GA, 512), (C_GR, 512)]
    win_chunks = [cc for (c0, w) in grp_cols for cc in range(c0, c0 + w, 256)]
    for ch in range(12):
        pos = win_chunks[ch]
        DMA(win_bf[:, :, pos:pos + 256], wada_v[:, :, ch * 256:(ch + 1) * 256], [], [("win", pos, pos + 256)],
            "wa%d" % ch, q="pool")
    cond_banks = [3, 4, 5]
    for ch in range(12):
        pos = win_chunks[ch]
        bk = cond_banks[ch % 3]
        for jj in range(2):
            for kc in range(8):
                MM(bank(bk, jj * 18, jj * 18 + 18), win_bf[:, kc, pos + jj * 128:pos + (jj + 1) * 128], scT[:, kc, :],
                   kc == 0, kc == 7, [("win", pos, pos + 256), "scT"], [PSK(bk)])
        for jj in range(2):
            j = ch * 2 + jj
            TS("dve", condT[:, j, :], bank(bk, jj * 18, jj * 18 + 18), bcol[:, j:j + 1], None, ALU.add, None,
               [PSK(bk), "bcol"], ["condT"])
    TS("dve", condT[:, 8:16, :], condT[:, 8:16, :], 1.0, None, ALU.add, None, ["condT"], ["condT"])
    for k, cc in enumerate(win_chunks):
        DMA(win_bf[:, :, cc:cc + 256], win_v[:, :, cc:cc + 256], [], [("win", cc, cc + 256)], "wi%d" % k, q="pool")
    for c in range(4):
        cc = c * 256
        DMA(wout_bf[:, :, cc:cc + 256], wout_v[:, :, cc:cc + 256], [], [("wout", cc, cc + 256)], "wo%d" % c, q="pool")

    def gate_tile(dst, dstK, src_fn):
        for j in range(8):
            sl = j % 2
            src = src_fn(j)
            dstv = gbt[:, sl, :] if len(src.shape) == 2 else gbt[:, sl, :].rearrange("p (s i) -> p s i", s=16)
            CP("dve", dstv, src, ["condT"], [("gbt", sl * 128, sl * 128 + 128)])
            TR(ps[:, 512 + j * 128:512 + (j + 1) * 128], gbt[:, sl, :], identf[:],
               [("gbt", sl * 128, sl * 128 + 128), "identf"], [PSK(1 + j // 4)])
        CP("act", dst, ps[:, 512:1536], [PSK(1), PSK(2)], [dstK])

    for b in range(NB):
        gate_tile(gate_bc[:, b, :], ("gate_bc", b * 1024, (b + 1) * 1024),
                  lambda j, b=b: condT[:, 16 + j, b:b + 1].broadcast_to([P, 128]))

    if do_sample:
        gate_tile(gate_bc[:, 2, :], ("gate_bc", 2048, 3072),
                  lambda j: condT[:, 16 + j, 2:18].unsqueeze(2).broadcast_to([P, 16, 8]))

    pending_hooks = []

    def win_loop():
        for f in pending_hooks:
            f()

    def xK_(slot):
        return ("xt", slot * 1024, (slot + 1) * 1024)

    def hK_(slot):
        return ("hT", slot * 1024, (slot + 1) * 1024)

    def f_cast(xslot):
        CP("act", xb[:], xt[:, xslot, :], [xK_(xslot)], ["xb"])

    def f_tr(bk=0):
        t0b = bankb(bk)
        for c in range(8):
            TR(t0b[:, c * 128:(c + 1) * 128], xb[:, c * 128:(c + 1) * 128], identb[:], ["xb", "identb"], [PSK(bk)])

    def f_hevac(hslot, mode, b, bk=0):
        t0b = bankb(bk)
        hK = hK_(hslot)
        if mode == "p":
            for c in range(8):
                hKc = hK
                if True:
                    ACT(hT[:, hslot, c, :], t0b[:, c * 128:(c + 1) * 128], AF.Identity, [PSK(bk), "condT"], [hKc],
                        bias=condT[:, c, b:b + 1], scale=condT[:, 8 + c, b:b + 1])
                else:
                    TS("dve", hT[:, hslot, c, :], t0b[:, c * 128:(c + 1) * 128], condT[:, 8 + c, b:b + 1],
                       condT[:, c, b:b + 1], ALU.mult, ALU.add, [PSK(bk), "condT"], [hKc])
        else:
            tmp = ot[:, 1, :].rearrange("p (c s i) -> p c s i", c=8, s=16)
            sc1e = condT[:, 8:16, 2:18].unsqueeze(3).broadcast_to([P, 8, 16, 8])
            she = condT[:, 0:8, 2:18].unsqueeze(3).broadcast_to([P, 8, 16, 8])
            TT("dve", tmp, t0b.rearrange("p (c s i) -> p c s i", c=8, s=16), sc1e, ALU.mult,
               [PSK(bk), "condT"], [("ot", 1024, 2048)])
            TT("dve", hT[:, hslot, :, :].rearrange("p c (s i) -> p c s i", s=16), tmp, she, ALU.add,
               [("ot", 1024, 2048), "condT"], [hK])

    def zgroup(hslot, bk, c0, w):
        for kc in range(8):
            MM(bank(bk, 0, w), hT[:, hslot, kc, :], win_bf[:, kc, c0:c0 + w], kc == 0, kc == 7,
               [hK_(hslot), ("win", c0, c0 + w)], [PSK(bk)])

    def rotary(bk, dst, dstK, tab_ap, tabK, rA=rA, rB=rB, nmA="rA", nmB="rB"):
        cc_b = tab_ap[:, 0:128].unsqueeze(1).broadcast_to([P, 4, 128])
        ss0 = tab_ap[:, 128:192].unsqueeze(1).broadcast_to([P, 4, 64])
        ss1 = tab_ap[:, 192:256].unsqueeze(1).broadcast_to([P, 4, 64])
        z4 = bank(bk).rearrange("p (h t d) -> p h t d", h=4, t=2)
        TT("dve", rA[:].rearrange("p (h d) -> p h d", h=4), bank(bk).rearrange("p (h d) -> p h d", h=4), cc_b,
           ALU.mult, [PSK(bk), tabK], [nmA])
        rB4 = rB[:].rearrange("p (h t d) -> p h t d", h=4, t=2)
        TT("dve", rB4[:, :, 0, :], z4[:, :, 1, :], ss0, ALU.mult, [PSK(bk), tabK], [(nmB, 0, 1)])
        TT("dve", rB4[:, :, 1, :], z4[:, :, 0, :], ss1, ALU.mult, [PSK(bk), tabK], [(nmB, 1, 2)])
        TT("pool", dst[:].rearrange("p h d -> p (h d)"), rA[:], rB[:], ALU.add, [nmA, nmB], [dstK])

    def z_qr(hslot, bk, tab_ap, tabK):
        zgroup(hslot, bk, C_QR, 512)
        rotary(bk, qr_b, "qr_b", tab_ap, tabK)

    def z_kr(hslot, bk, tab_ap, tabK, kdec_ap):
        zgroup(hslot, bk, C_KR, 512)
        rotary(bk, kr_b, "kr_b", tab_ap, tabK, rA2, rB2, "rA2", "rB2")
        TT("pool", ks_b[:], kr_b[:], kdec_ap.unsqueeze(2).broadcast_to([P, 4, 128]), ALU.mult,
           ["kr_b", "smalls"], ["ks_b"])

    def z_qa(hslot, bk):
        zgroup(hslot, bk, C_QA, 512)
        CP("act", qa_b[:], bank(bk), [PSK(bk)], ["qa_b"])

    def z_kv(hslot, bk, vslot, want_f32):
        zgroup(hslot, bk, C_K, 256)
        CP("act", ka_b[:], bank(bk, 0, 128), [PSK(bk)], ["ka_b"])
        CP("act", va1[:, vslot, :, 0:64], bank(bk, 128, 256).rearrange("p (h d) -> p h d", h=2), [PSK(bk)],
           [("va1", vslot, vslot + 1)])
        if want_f32:
            CP("act", kv_f[:], bank(bk, 0, 256), [PSK(bk)], ["kv_f"])

    def z_vr(hslot, bk):
        zgroup(hslot, bk, C_VR, 512)
        CP("act", vr_b[:].rearrange("p h d -> p (h d)"), bank(bk), [PSK(bk)], ["vr_b"])

    def z_ga(hslot, bk):
        zgroup(hslot, bk, C_GA, 512)
        ACT(tg[:], bank(bk), AF.Tanh, [PSK(bk)], ["tg"], scale=0.5)
        STT(sg_a[:], tg[:], 1.0, bank(bk), ALU.add, ALU.mult, ["tg", PSK(bk)], ["sg_a"])

    def z_gr(hslot, bk):
        zgroup(hslot, bk, C_GR, 512)
        ACT(tg2[:], bank(bk), AF.Tanh, [PSK(bk)], ["tg2"], scale=0.5)
        STT(sg_r[:], tg2[:], 1.0, bank(bk), ALU.add, ALU.mult, ["tg2", PSK(bk)], ["sg_r"])

    def m_tr(kslot, b1, b2):
        m_tr_banks(kslot, b1, b2, 0)

    def m_tr_banks(kslot, b1, b2, b0):
        t2 = bankb(b2)
        for h in range(4):
            TR(t2[:, h * 128:(h + 1) * 128], qr_b[:, h, :], identb[:], ["qr_b", "identb"], [PSK(b2)])
        for h in range(4):
            TR(t2[:, (4 + h) * 128:(5 + h) * 128], kr_b[:, h, :], identb[:], ["kr_b", "identb"], [PSK(b2)])
        t1 = bankb(b1)
        for h in range(8):
            TR(t1[0:64, h * 128:(h + 1) * 128], qa_b[:, h * 64:(h + 1) * 64], identb[:], ["qa_b", "identb"], [PSK(b1)])
        t0 = bankb(b0)
        for h in range(2):
            TR(t0[0:64, h * 128:(h + 1) * 128], ka_b[:, h * 64:(h + 1) * 64], identb[:], ["ka_b", "identb"], [PSK(b0)])
        CP("act", qaT[0:64, :, :].rearrange("p a b -> p (a b)"), t1[0:64, :], [PSK(b1)], ["qaT"])
        CP("act", kaT[0:64, kslot, :, :].rearrange("p a b -> p (a b)"), t0[0:64, 0:256], [PSK(b0)],
           [("kaT", kslot, kslot + 1)])
        CP("dve", qkrT[:].rearrange("p a b -> p (a b)"), t2, [PSK(b2)], ["qkrT"])

    def score(bk, lhsT, lhsK, h, neg_ap, negK, es_):
        MM(bank(bk), lhsT, qaT[0:64, 4 * h:4 * h + 4, :].rearrange("p a b -> p (a b)"), True, False,
           [lhsK, "qaT"], [PSK(bk)])
        MM(bank(bk), identb[:], neg_ap.unsqueeze(1).broadcast_to([P, 4, 128]), False, True,
           ["identb", negK], [PSK(bk)])
        ACT(Eb[:, es_, :], bank(bk), AF.Exp, [PSK(bk)], [("Eb", es_, es_ + 1)], scale=0.125)

    def attn_finish(o_ap_fn, oK_fn, heads, den_ap=None):
        h0, h1 = heads[0], heads[-1] + 1
        if den_ap is not None:
            TT("dve", den[:, h0:h1], den_ap, esink[:, h0:h1], ALU.add, [oK_fn(h0), "esink"], [("scr", h0, h1)])
        else:
            for hd in heads:
                TT("dve", den[:, hd:hd + 1], o_ap_fn(hd)[:, 64:65], esink[:, hd:hd + 1], ALU.add,
                   [oK_fn(hd), "esink"], [("scr", hd, hd + 1)])
        S.add("dve", lambda e: e.reciprocal(rden[:, h0:h1], den[:, h0:h1]),
              [("scr", h0, h1)], [("scr", 8 + h0, 8 + h1)])
        for hd in heads:
            STT(mixed[:, hd * 64:(hd + 1) * 64], o_ap_fn(hd)[:, 0:64], rden[:, hd:hd + 1],
                sg_a[:, hd * 64:(hd + 1) * 64], ALU.mult, ALU.mult,
                [oK_fn(hd), ("scr", 8 + hd, 9 + hd), "sg_a"], [("mixed", hd * 64, (hd + 1) * 64)])

    def gn_a(u_ap_fn, uK_fn, epsp_ap):
        for h in range(4):
            S.add("dve", lambda e, h=h: e.bn_stats(st_gn[:, h * 6:(h + 1) * 6], u_ap_fn(h)),
                  [uK_fn(h)], [("scr", 16 + h * 6, 22 + h * 6)])
            S.add("dve", lambda e, h=h: e.bn_aggr(mv_gn[:, h * 2:(h + 1) * 2], st_gn[:, h * 6:(h + 1) * 6]),
                  [("scr", 16 + h * 6, 22 + h * 6)], [("scr", 40 + 2 * h, 42 + 2 * h)])
        mv3 = mv_gn.rearrange("p (h t) -> p h t", t=2)
        TT("dve", vpe, mv3[:, :, 1], epsp_ap, ALU.add, [("scr", 40, 48), "smalls"], [("scr", 48, 52)])
        TT("pool", rstd, vpe, smalls[:, 16:20], ALU.pow, [("scr", 48, 52), "smalls"], [("scr", 52, 56)])

    def gn_b(u_ap_fn, uK_fn):
        mv3 = mv_gn.rearrange("p (h t) -> p h t", t=2)
        STT(nbg, mv3[:, :, 0], -1.0, rstd, ALU.mult, ALU.mult, [("scr", 40, 48), ("scr", 52, 56)], [("scr", 56, 60)])
        for h in range(4):
            ACT(yn[:, h, :], u_ap_fn(h), AF.Identity, [uK_fn(h), ("scr", 52, 60)], [("yn", h, h + 1)],
                bias=nbg[:, h:h + 1], scale=rstd[:, h:h + 1])

    def gn_c():
        TT("pool", yn[:].rearrange("p h d -> p (h d)"), yn[:].rearrange("p h d -> p (h d)"), gnwh[:], ALU.mult,
           ["yn", "gnwh"], ["yn"])
        TT("pool", mixed[:, 512:1024], yn[:].rearrange("p h d -> p (h d)"), sg_r[:], ALU.mult,
           ["yn", "sg_r"], [("mixed", 512, 1024)])

    def b_tr():
        t0b = bankb(0)
        for c in range(8):
            TR(t0b[:, c * 128:(c + 1) * 128], mixed[:, c * 128:(c + 1) * 128], identb[:], ["mixed", "identb"], [PSK(0)])
        CP("act", mixT[:].rearrange("p a b -> p (a b)"), t0b, [PSK(0)], ["mixT"])

    def b_proj():
        for nh in range(2):
            for kc in range(8):
                MM(bank(1 + nh), mixT[:, kc, :], wout_bf[:, kc, nh * 512:(nh + 1) * 512], kc == 0, kc == 7,
                   ["mixT", ("wout", nh * 512, nh * 512 + 512)], [PSK(1 + nh)])

    def b_ln_a(xslot, oslot, gate_ap, gateK, defer_r=False):
        oK = ("ot", oslot * 1024, (oslot + 1) * 1024)
        o = ot[:, oslot, :]
        TT("dve", o, ps[:, 512:1536], gate_ap, ALU.mult, [PSK(1), PSK(2), gateK], [oK])
        if not defer_r:
            STT(o, xt[:, xslot, :], ALPHA, o, ALU.mult, ALU.add, [xK_(xslot), oK], [oK])

    def b_ln_r(xslot, oslot):
        oK = ("ot", oslot * 1024, (oslot + 1) * 1024)
        o = ot[:, oslot, :]
        STT(o, xt[:, xslot, :], ALPHA, o, ALU.mult, ALU.add, [xK_(xslot), oK], [oK])

    def b_ln_a2(oslot):
        oK = ("ot", oslot * 1024, (oslot + 1) * 1024)
        o = ot[:, oslot, :]
        for k in range(2):
            S.add("dve", lambda e, k=k: e.bn_stats(st_ln[:, k * 6:(k + 1) * 6], o[:, k * 512:(k + 1) * 512]),
                  [oK], [("scr2", k * 6, k * 6 + 6)])
        S.add("dve", lambda e: e.bn_aggr(mv_ln, st_ln), [("scr2", 0, 12)], [("scr2", 12, 14)])
        TS("dve", vpl, mv_ln[:, 1:2], LN_EPS, None, ALU.add, None, [("scr2", 12, 14)], [("scr2", 14, 15)])
        TT("pool", rsl, vpl, smalls[:, 16:17], ALU.pow, [("scr2", 14, 15), "smalls"], [("scr2", 15, 16)])

    def b_ln_b(oslot, out_dram, osem):
        oK = ("ot", oslot * 1024, (oslot + 1) * 1024)
        o = ot[:, oslot, :]
        STT(nbl, mv_ln[:, 0:1], -1.0, rsl, ALU.mult, ALU.mult, [("scr2", 12, 14), ("scr2", 15, 16)], [("scr2", 16, 17)])
        ACT(o, o, AF.Identity, [oK, ("scr2", 15, 17)], [oK], bias=nbl, scale=rsl)
        TT("pool", o, o, lnw_bc[:], ALU.mult, [oK, "lnw_bc"], [oK])
        TT("pool", o, o, lnb_bc[:], ALU.add, [oK, "lnb_bc"], [oK])
        DMA(out_dram, o, [oK], [], osem)

    def sample_phase(hook_early=None, hook_mid=None, hook_late=None):
        NSL = 7
        Ssl = [arF[:, s * 256:(s + 1) * 256].rearrange("p (h e) -> p h e", h=2) for s in range(NSL)]
        SslK = [("arF", s * 256, (s + 1) * 256) for s in range(NSL)]
        Snw = [arF[:, 1792 + s * 256:1792 + (s + 1) * 256].rearrange("p (h e) -> p h e", h=2) for s in range(2)]
        SnwK = [("arF", 1792 + s * 256, 1792 + (s + 1) * 256) for s in range(2)]
        maskRs = arF[:, 2304:2816]
        gate_s = gate_bc[:, 2, :]
        Sbs = [arB[:, s * 256:(s + 1) * 256].rearrange("p (h e) -> p h e", h=2) for s in range(3)]
        SbsK = [("arB", s * 256, (s + 1) * 256) for s in range(3)]
        ksx = [arB[:, 768 + s * 256:768 + (s + 1) * 256].rearrange("p (h d) -> p h d", h=2) for s in range(2)]
        ksxK = [("arB", 768 + s * 256, 768 + (s + 1) * 256) for s in range(2)]
        qsx = arB[:, 4096:8192].rearrange("p (h s t) -> p h s t", h=2, s=16)
        qsxK = ("arB", 4096, 8192)
        Mn = arB[:, 10272:10400]
        Mc = arB[:, 10400:12448].rearrange("p (s t) -> p s t", s=16)
        selT = arB[:, 12448:14496].rearrange("p (s t) -> p s t", s=16)
        cbsK = ("arB", 10272, 14496)

        DMA(arF[:, 2304:2816], d_cfs[:, 0:512], [], [("arF", 2304, 2816)], "c2")
        DMA(kws[:, 0:120, :], ck[:, 8:128, :], [], [], "wo")
        DMA(vws[:, 0:120, :], cv[:, 8:128, :], [], [], "wo")

        m_tr(0, 1, 2)
        for s in range(NS):
            DMA(kws[s, 120:128, :], kv_f[8 * s:8 * s + 8, 0:128], ["kv_f"], [], "wo")
            DMA(vws[s, 120:128, :], kv_f[8 * s:8 * s + 8, 128:256], ["kv_f"], [], "wo")

        obanks = [5, 6, 7, 2]
        sbanks = [3, 4]
        ecnt = [0]

        def nxt():
            i = ecnt[0]
            ecnt[0] += 1
            return sbanks[i % 2], i % 4

        for h in range(2):
            def do_score(k):
                bk, es_ = nxt()
                if k == 0:
                    score(bk, kaT[0:64, 0, h, :], ("kaT", 0, 1), h, Mn, cbsK, es_)
                else:
                    sq = k - 1
                    eK = ("Eb", es_, es_ + 1)
                    MEMSET("pool", Eb[:, es_, :], 0.0, [eK])
                    sc = bank(bk, 0, 32)
                    MM(sc, kTc_all[0:64, sq, h, :], qaT[0:64, 4 * h:4 * h + 4, 8 * sq:8 * sq + 8], True, False,
                       [kTcK, "qaT"], [PSK(bk)])
                    MM(sc, identb[:], Mc[:, sq, 8 * sq:8 * sq + 8].unsqueeze(1).broadcast_to([P, 4, 8]), False, True,
                       ["identb", cbsK], [PSK(bk)])
                    ACT(Eb[:, es_, :].rearrange("p (g t) -> p g t", g=4)[:, :, 8 * sq:8 * sq + 8],
                        sc.rearrange("p (g i) -> p g i", g=4), AF.Exp, [PSK(bk)], [eK], scale=0.125)
                return es_

            def do_pv(k, es_):
                vap, vK = (va1[:, 0, h, :], ("va1", 0, 1)) if k == 0 else (va1c_all[:, k - 1, h, :], va1cK)
                for g in range(4):
                    MM(bank(obanks[g], 0, 65), Eb[:, es_, g * 128:(g + 1) * 128], vap, k == 0, k == NS,
                       [("Eb", es_, es_ + 1), vK], [PSK(obanks[g])])

            pend_e = [do_score(0), do_score(1)]
            for k in range(NS + 1):
                if k + 2 <= NS:
                    pend_e.append(do_score(k + 2))
                do_pv(k, pend_e[k])
            attn_finish(lambda hd: bank(obanks[hd % 4], 0, 65), lambda hd: PSK(obanks[hd % 4]),
                        [4 * h + g for g in range(4)])

        if hook_early is not None:
            hook_early()
        for hh in range(4):
            MM(bank(1, hh * 128, (hh + 1) * 128), qkrT[:, 4 + hh, :], qkrT[:, hh, :], True, True, ["qkrT"], [PSK(1)])
        TT("dve", Pr[:].rearrange("p h d -> p (h d)"), bank(1), maskRs, ALU.mult, [PSK(1), ("arF", 2304, 2816)], ["Pr"])
        ubanks = [5, 6, 7, 2]
        dsbanks = [3, 4]
        it = 0
        for hp in range(2):
            for hh in range(2):
                TT("dve", qsx[:, hh, :, :], qkrT[:, 2 * hp + hh, :].unsqueeze(1).broadcast_to([P, 16, 128]), selT,
                   ALU.mult, ["qkrT", cbsK], [qsxK])
            for hh in range(2):
                hd = 2 * hp + hh
                MM(bank(ubanks[hd], 0, 128), Pr[:, hd, :], vr_b[:, hd, :], True, False, ["Pr", "vr_b"], [PSK(ubanks[hd])])
            for s in range(NS):
                s3 = it % NSL
                sb3 = it % 3
                s2 = it % 2
                PF = NSL - 1
                if it == 0:
                    for q in range(PF):
                        DMA(Ssl[q], sr[q, 0:2].rearrange("h d e -> d h e"), [], [SslK[q]], "ss%d" % q)
                nx = it + PF
                if nx < 2 * NS:
                    hpn, sn = nx // NS, nx % NS
                    DMA(Ssl[nx % NSL], sr[sn, 2 * hpn:2 * hpn + 2].rearrange("h d e -> d h e"), [], [SslK[nx % NSL]],
                        "ss%d" % (nx % NSL))
                it += 1
                CP("act", Sbs[sb3], Ssl[s3], [SslK[s3]], [SbsK[sb3]])
                ACT(ksx[s2], ks_b[:, 2 * hp:2 * hp + 2, :], AF.Identity, ["ks_b", "smalls"], [ksxK[s2]],
                    scale=smalls[:, 20 + s:21 + s])
                for hh in range(2):
                    hd = 2 * hp + hh
                    MM(bank(ubanks[hd], 0, 128), qsx[:, hh, s, :], Sbs[sb3][:, hh, :], False, s == NS - 1,
                       [qsxK, SbsK[sb3]], [PSK(ubanks[hd])])
                db = dsbanks[s2]
                for hh in range(2):
                    hd = 2 * hp + hh
                    MM(bank(db, hh * 128, (hh + 1) * 128), ksx[s2][:, hh, :], vr_b[:, hd, :], True, True,
                       [ksxK[s2], "vr_b"], [PSK(db)])
                for hh in range(2):
                    hd = 2 * hp + hh
                    STT(Snw[s2][:, hh, :], Ssl[s3][:, hh, :], float(GAM[hd] ** 8), bank(db, hh * 128, (hh + 1) * 128),
                        ALU.mult, ALU.add, [SslK[s3], PSK(db)], [SnwK[s2]])
                DMA(srs[s, 2 * hp:2 * hp + 2].rearrange("h d e -> d h e"), Snw[s2], [SnwK[s2]], [], "so%d" % s2)
        ufn = lambda h: bank(ubanks[h], 0, 128)
        uKf = lambda h: PSK(ubanks[h])
        gn_a(ufn, uKf, smalls[:, 8:12])
        gn_b(ufn, uKf)
        gn_c()
        if hook_mid is not None:
            hook_mid()
        b_tr()
        b_proj()
        if hook_late is not None:
            hook_late()
        b_ln_a(0, 0, gate_s, ("gate_bc", 2048, 3072))
        b_ln_a2(0)
        b_ln_b(0, ys, "yo0")

    if do_sample and not phase0_only:
        tabs0 = tabt[:, 0, :]
        tabs0K = ("tabt", 0, 1)
        f_cast(0)
        f_tr()
        f_hevac(0, "s", None)
        pending_hooks.extend([
            lambda: z_qr(0, 7, tabs0, tabs0K),
            lambda: z_kr(0, 3, tabs0, tabs0K, smalls[:, 12:16]),
            lambda: z_qa(0, 4),
            lambda: z_kv(0, 5, 0, True),
            lambda: z_vr(0, 6),
            lambda: z_ga(0, 7),
            lambda: z_gr(0, 3),
        ])
    win_loop()
    DMA(xt[:, 1, :], xp[0, 0:128, :], [], [("xt", 1024, 2048), ("xtb", 0, 2)], "x1")

    tiles = [(b, n) for b in range(NB) for n in range(NT)]
    NTILES = len(tiles) if ntiles is None else ntiles
    if phase0_only:
        NTILES = 0

    def xslot_(i):
        return (i + 1) % 3

    def load_x(i):
        b, n = tiles[i]
        sl = xslot_(i)
        DMA(xt[:, sl, :], xp[b, n * 128:(n + 1) * 128, :], [], [xK_(sl)], "x%d" % sl)

    def load_tab(i):
        b, n = tiles[i]
        sl = i % 2
        DMA(tabt[:, sl, :], d_tabp[n], [], [("tabt", sl, sl + 1)], "tb%d" % sl)

    def tabK_(i):
        return ("tabt", i % 2, i % 2 + 1)

    def fa(i):
        b, n = tiles[i]
        f_cast(xslot_(i))
        f_tr()
        f_hevac(i % 2, "p", b)

    SB = {(0, "c"): 6, (0, "p"): 7, (1, "c"): 4, (1, "p"): 5}
    ES = {(0, "c"): 0, (0, "p"): 1, (1, "c"): 2, (1, "p"): 3}
    OB = [6, 4]

    def o_ap(hd):
        return bank(OB[hd // 4], (hd % 4) * 65, (hd % 4) * 65 + 65)

    def o_K(hd):
        return PSK(OB[hd // 4])

    def u_ap(h):
        return bank(2, h * 128, (h + 1) * 128)

    def u_K(h):
        return PSK(2)

    pending_ln = [None]
    def z_first(j, hs):
        z_qr(hs, 7, tabt[:, j % 2, :], tabK_(j))
        z_kr(hs, 5, tabt[:, j % 2, :], tabK_(j), smalls[:, 4:8])

    def z_all(j, hs):
        bj, nj = tiles[j]
        z_qa(hs, 1)
        z_kv(hs, 0, j % 2, nj == NT - 1)
        z_vr(hs, 3)

    def z_rest(hs):
        z_ga(hs, 1)
        z_gr(hs, 5)

    def m_tr2(kslot):
        m_tr_banks(kslot, 4, 6, 3)

    def iteration(i):
        S.cur = i
        b, n = tiles[i]
        ks, kp = i % 2, (i + 1) % 2
        hs_next = (i + 1) % 2
        has_next = i + 1 < NTILES
        kinds = ["c", "p"] if n > 0 else ["c"]
        pend = pending_ln[0]
        pending_ln[0] = None
        if pend is not None:
            pend[0]()
        if i + 2 < NTILES:
            load_tab(i + 2)
        if n == NT - 1:
            DMA(kwp[b], kv_f[:, 0:128], ["kv_f"], [], "wk")
            DMA(vwp[b], kv_f[:, 128:256], ["kv_f"], [], "wk")
        for hh in range(4):
            MM(bank(3, hh * 128, (hh + 1) * 128), ks_b[:, hh, :], vr_b[:, hh, :], True, True, ["ks_b", "vr_b"], [PSK(3)])
        for hh in range(4):
            MM(bank(0, hh * 128, (hh + 1) * 128), qkrT[:, 4 + hh, :], qkrT[:, hh, :], True, True, ["qkrT"], [PSK(0)])
        TT("dve", Pr[:].rearrange("p h d -> p (h d)"), bank(0), maskR[:].rearrange("p h d -> p (h d)"), ALU.mult,
           [PSK(0), "maskR"], ["Pr"])
        if n == 0:
            CP("dve", Sst[:].rearrange("p h e -> p (h e)"), bank(3), [PSK(3)], ["Sst"])
        else:
            for hh in range(4):
                STT(Sst[:, hh, :], Sst[:, hh, :], float(GAM[hh] ** 128), bank(3, hh * 128, (hh + 1) * 128),
                    ALU.mult, ALU.add, ["Sst", PSK(3)], ["Sst"])
        if n == NT - 1:
            DMA(srp[b].rearrange("h d e -> d h e"), Sst[:], ["Sst"], [], "wS")
        for h in range(2):
            for kind in kinds:
                slot = ks if kind == "c" else kp
                neg = cbp[:, 1, :] if kind == "c" else cbp[:, 0, :]
                score(SB[(h, kind)], kaT[0:64, slot, h, :], ("kaT", slot, slot + 1), h, neg, "cbp", ES[(h, kind)])
        if pend is not None:
            pend[1]()
        if i + 2 < NTILES:
            load_x(i + 2)
        for hh in range(4):
            if n > 0:
                MM(u_ap(hh), qkrT[:, hh, :], Sbf[:, hh, :], True, False, ["qkrT", "Sbf"], [PSK(2)])
            MM(u_ap(hh), Pr[:, hh, :], vr_b[:, hh, :], n == 0, True, ["Pr", "vr_b"], [PSK(2)])
        if n != NT - 1:
            CP("act", Sbf[:].rearrange("p h e -> p (h e)"), Sst[:].rearrange("p h e -> p (h e)"), ["Sst"], ["Sbf"])
        gn_a(u_ap, u_K, smalls[:, 0:4])
        if has_next:
            z_qr(hs_next, 7, tabt[:, (i + 1) % 2, :], tabK_(i + 1))
        gn_b(u_ap, u_K)
        gn_c()
        if has_next:
            z_kr(hs_next, 5, tabt[:, (i + 1) % 2, :], tabK_(i + 1), smalls[:, 4:8])
        for h in range(2):
            for g in range(4):
                o = o_ap(4 * h + g)
                e_ = ES[(h, "c")]
                MM(o, Eb[:, e_, g * 128:(g + 1) * 128], va1[:, ks, h, :], True, n == 0,
                   [("Eb", e_, e_ + 1), ("va1", ks, ks + 1)], [PSK(OB[h])])
                if n > 0:
                    e_ = ES[(h, "p")]
                    MM(o, Eb[:, e_, g * 128:(g + 1) * 128], va1[:, kp, h, :], False, True,
                       [("Eb", e_, e_ + 1), ("va1", kp, kp + 1)], [PSK(OB[h])])
        for h in range(2):
            dview = bank(OB[h], 0, 260).rearrange("p (g c) -> p g c", g=4)[:, :, 64]
            attn_finish(o_ap, o_K, [4 * h + g for g in range(4)], dview)
        if i + 2 < NTILES:
            f_cast(xslot_(i + 2))
        if has_next:
            z_all(i + 1, hs_next)
        if i + 2 < NTILES:
            f_tr(7)
            f_hevac(i % 2, "p", tiles[i + 2][0], 7)
        if pend is not None:
            pend[2]()
        if has_next:
            z_rest(hs_next)
        osl = (i + 1) % 2
        b_tr()
        if has_next:
            m_tr2((i + 1) % 2)
        b_proj()
        xs_i = xslot_(i)
        pending_ln[0] = (
            lambda: b_ln_a(xs_i, osl, gate_bc[:, b, :], ("gate_bc", b * 1024, (b + 1) * 1024), True),
            lambda: (b_ln_r(xs_i, osl), b_ln_a2(osl)),
            lambda: b_ln_b(osl, yp[b, n * 128:(n + 1) * 128, :], "yo%d" % osl),
        )

    def pro_early():
        load_tab(0)
        if NTILES > 1:
            load_x(1)
            load_tab(1)
        fa(0)

    def pro_mid():
        tb, tk = tabt[:, 0, :], tabK_(0)
        z_qr(0, 0, tb, tk)
        z_kr(0, 1, tb, tk, smalls[:, 4:8])
        z_qa(0, 3)
        z_kv(0, 4, 0, False)
        z_vr(0, 3)

    def pro_late():
        z_ga(0, 0)
        z_gr(0, 3)
        m_tr2(0)
        if NTILES > 1:
            f_cast(xslot_(1))
            f_tr(7)
            f_hevac(1, "p", tiles[1][0], 7)

    if do_sample and not phase0_only:
        if NTILES > 0:
            sample_phase(pro_early, pro_mid, pro_late)
        else:
            sample_phase()
    elif NTILES > 0:
        pro_early()
        pro_mid()
        pro_late()
    for i in range(NTILES):
        iteration(i)
    if pending_ln[0] is not None:
        for f in pending_ln[0]:
            f()
    sem_names = ["pe", "act", "dve", "pool"] + sorted(k for k in S.cnt if k.startswith("D:"))
    sems = {}
    for nm in sem_names:
        sems[nm] = es.enter_context(nc.semaphore("s_" + nm.replace(":", "_")))
    GROUP = {"D:c0", "D:c1", "D:c2"}
    final_waits = [(k, v) for k, v in S.cnt.items() if k.startswith("D:")]

    def emit(name, e, tail=False):
        issued = {}
        for fn, waits, pid, inc, tag in S.ops[name]:
            for p, v in waits:
                if p in GROUP:
                    assert name != "sp" or issued.get(p, 0) == S.cnt[p], (p, v)
                    v = S.cnt[p]
                e.wait_ge(sems[p], v)
            ins = fn(e)
            ins.then_inc(sems[pid], inc)
            if TAGS is not None:
                try:
                    TAGS[ins.ins.name] = tag
                except Exception:
                    pass
            issued[pid] = issued.get(pid, 0) + inc
        if tail:
            for p, v in final_waits:
                e.wait_ge(sems[p], v)

    with nc.Block() as block:
        @block.tensor
        def _(e):
            emit("pe", e)

        @block.scalar
        def _(e):
            emit("act", e)

        @block.vector
        def _(e):
            emit("dve", e)

        @block.gpsimd
        def _(e):
            emit("pool", e)

        @block.sync
        def _(e):
            emit("sp", e, tail=True)

    es.close()
    return nc, tapl


_CACHE = {}


def _get_program(taps=None):
    key = tuple(sorted(taps)) if taps else ()
    if key not in _CACHE:
        _CACHE[key] = build_program(taps)
    return _CACHE[key]


def make_in_maps(x_prompt, x_sample, c_prompt, c_sample, cache_k_win, cache_v_win, state_ret,
                 w_ada, b_ada, w_in, attn_sinks, ret_gn_w, w_out, ln_w, ln_b):
    f = lambda a: np.ascontiguousarray(np.asarray(a, dtype=np.float32))
    consts = make_consts()
    shared = {
        "wada": f(w_ada[0]), "bada": f(b_ada[0]).reshape(24, 128), "win": f(w_in[0]),
        "sinks": f(attn_sinks[0]), "gnw": f(ret_gn_w[0]).reshape(4, 128), "wout": f(w_out[0]),
        "lnw": f(ln_w[0]), "lnb": f(ln_b[0]),
    }
    shared.update(consts)
    in_maps = []
    for c in range(NCORES):
        m = dict(shared)
        m["xp"] = f(x_prompt[NB * c:NB * (c + 1)])
        m["xs"] = f(x_sample[NS * c:NS * (c + 1)]).reshape(128, D)
        m["c18"] = np.concatenate([f(c_prompt[NB * c:NB * (c + 1)]), f(c_sample[NS * c:NS * (c + 1)])], axis=0)
        m["ck"] = f(cache_k_win[0, NS * c:NS * (c + 1)]).reshape(NS, 128, 128)
        m["cv"] = f(cache_v_win[0, NS * c:NS * (c + 1)]).reshape(NS, 128, 128)
        m["sr"] = f(state_ret[0, NS * c:NS * (c + 1)])
        in_maps.append(m)
    return in_maps


def kernel(**inputs):
    nc, _ = _get_program()
    in_maps = make_in_maps(**inputs)
    res = run_bass_kernel_spmd(nc, in_maps, core_ids=list(range(NCORES)))
    r = res.results
    cat = lambda k: np.concatenate([np.asarray(r[c][k]) for c in range(NCORES)], axis=0)
    y_p = cat("yp").reshape(16, 2048, D)
    y_s = cat("ys").reshape(128, 8, D)
    kwp = cat("kwp").reshape(1, 16, 128, 2, 64)
    vwp = cat("vwp").reshape(1, 16, 128, 2, 64)
    srp = cat("srp").reshape(1, 16, 4, 128, 128)
    kws = cat("kws").reshape(1, 128, 128, 2, 64)
    vws = cat("vws").reshape(1, 128, 128, 2, 64)
    srs = cat("srs").reshape(1, 128, 4, 128, 128)
    return (y_p.astype(np.float32), y_s.astype(np.float32), kwp.astype(np.float32), vwp.astype(np.float32),
            srp.astype(np.float32), kws.astype(np.float32), vws.astype(np.float32), srs.astype(np.float32))
```

```python
import math
from contextlib import ExitStack

import numpy as np
import ml_dtypes
import concourse.bass as bass
import concourse.mybir as mybir
from concourse.bass_utils import run_bass_kernel_spmd

F32 = mybir.dt.float32
BF16 = mybir.dt.bfloat16
AF = mybir.ActivationFunctionType
ALU = mybir.AluOpType

P = 128
D = 1024
DIN = 3328
NT = 16
NB = 2
NS = 16
NCORES = 8
PAST = 16384
C_QA, C_K, C_V, C_GA, C_QR, C_KR, C_VR, C_GR = 0, 512, 640, 768, 1280, 1792, 2304, 2816
LN_EPS = 1e-5
GN_EPS = 1e-5
ALPHA = float(2.0 ** 0.25)
GAM = [1.0 - 2.0 ** (-5.0 - h) for h in range(4)]
ENGS = ["pe", "act", "dve", "pool", "sp"]
BIG = 1 << 30


TAGS = None


class Sched:
    def __init__(self):
        self.ops = {e: [] for e in ENGS}
        self.cnt = {}
        self.acc = {}
        self.waited = {e: {} for e in ENGS}
        self.cur = None

    @staticmethod
    def _norm(a):
        if isinstance(a, str):
            return (a, 0, BIG)
        return a

    def add(self, eng, fn, reads=(), writes=(), dma=None):
        pid = eng if dma is None else "D:" + dma
        inc = 1 if dma is None else 16
        deps = {}

        def need(pv):
            if pv is None:
                return
            p, v = pv
            if p == "pe" and eng == "pe" and dma is None:
                return
            if deps.get(p, 0) < v:
                deps[p] = v

        reads = [self._norm(a) for a in reads]
        writes = [self._norm(a) for a in writes]
        for (nm, lo, hi) in reads:
            for (l2, h2), ent in self.acc.get(nm, {}).items():
                if l2 < hi and lo < h2:
                    need(ent["w"])
        for (nm, lo, hi) in writes:
            for (l2, h2), ent in self.acc.get(nm, {}).items():
                if l2 < hi and lo < h2:
                    need(ent["w"])
                    for p, v in ent["r"].items():
                        need((p, v))
        waits = []
        for p, v in deps.items():
            if self.waited[eng].get(p, 0) >= v:
                continue
            self.waited[eng][p] = v
            waits.append((p, v))
        val = self.cnt.get(pid, 0) + inc
        self.cnt[pid] = val
        tag = ""
        if TAGS is not None:
            import sys as _sys
            f = _sys._getframe(1)
            names = []
            while f is not None and f.f_code.co_name != "build_program":
                names.append(f.f_code.co_name)
                f = f.f_back
            tag = "/".join(reversed(names[1:])) + (":" + str(self.cur) if self.cur is not None else "")
        self.ops[eng].append((fn, waits, pid, inc, tag))
        for (nm, lo, hi) in reads:
            ent = self.acc.setdefault(nm, {}).setdefault((lo, hi), {"w": None, "r": {}})
            ent["r"][pid] = val
        for (nm, lo, hi) in writes:
            ent = self.acc.setdefault(nm, {}).setdefault((lo, hi), {"w": None, "r": {}})
            ent["w"] = (pid, val)
            ent["r"] = {}


def _bf(a):
    return np.asarray(a, dtype=np.float32).astype(ml_dtypes.bfloat16)


def make_consts():
    c = {}
    c["identb"] = _bf(np.eye(128))
    c["identf"] = np.eye(128, dtype=np.float32)
    half = 64
    inv = (np.float32(10000.0) ** (-np.arange(half, dtype=np.float32) / np.float32(half))).astype(np.float32)

    def tab(pos):
        ang = (pos.astype(np.float32)[:, None] * inv[None, :]).astype(np.float32)
        cs = np.cos(ang).astype(np.float32)
        sn = np.sin(ang).astype(np.float32)
        return np.concatenate([cs, cs, -sn, sn], axis=1).astype(np.float32)

    pos_p = np.arange(NT * 128)
    c["tabp"] = tab(pos_p).reshape(NT, 128, 256)
    pos_s = PAST + (np.arange(128) % 8)
    tabs = tab(pos_s)
    g = np.array(GAM, dtype=np.float64)
    rs = 1.0 / math.sqrt(128.0)
    j = np.arange(128)
    tri = (j[None, :] >= j[:, None]).astype(np.float64)
    mR = tri[:, None, :] * (g[None, :, None] ** (-(j[:, None, None] + 1.0))) * rs
    c["maskR"] = mR.reshape(128, 512).astype(np.float32)
    sj, jj = j // 8, j % 8
    same = (sj[:, None] == sj[None, :])
    tri_s = same & (jj[None, :] >= jj[:, None])
    mRs = tri_s[:, None, :] * (g[None, :, None] ** (-(jj[:, None, None] + 1.0))) * rs
    maskRs = mRs.reshape(128, 512).astype(np.float32)
    sm = np.zeros((128, 36), dtype=np.float64)
    sm[:, 0:4] = GN_EPS / (g[None, :] ** (2.0 * (j[:, None] + 1.0)))
    sm[:, 4:8] = (g[None, :] ** (127.0 - j[:, None])) * rs
    sm[:, 8:12] = GN_EPS / (g[None, :] ** (2.0 * (jj[:, None] + 1.0)))
    sm[:, 12:16] = (g[None, :] ** (7.0 - jj[:, None])) * rs
    sm[:, 16:20] = -0.5
    sm[:, 20:36] = (sj[:, None] == np.arange(16)[None, :])
    c["smalls"] = sm.astype(np.float32)
    mprev = (j[:, None] >= j[None, :])
    mcur = (j[:, None] <= j[None, :])
    NEG = -30000.0
    neg = lambda m: np.where(m, 0.0, NEG)
    c["cbp"] = _bf(np.stack([neg(mprev), neg(mcur)], axis=1).reshape(128, 256))
    Mn = same & (jj[:, None] <= jj[None, :])
    Mc = (np.arange(16)[None, :, None] == sj[None, None, :]) & (j[:, None, None] >= jj[None, None, :])
    selT = np.broadcast_to((np.arange(16)[None, :, None] == sj[None, None, :]), (128, 16, 128))
    c["cbs"] = _bf(np.concatenate([neg(Mn).reshape(128, 128), neg(Mc).reshape(128, 2048), selT.reshape(128, 2048)], axis=1))
    c["cfs"] = np.concatenate([maskRs, tabs], axis=1).astype(np.float32)
    return c


def build_program(taps=None, ntiles=None, do_sample=True, phase0_only=False):
    nc = bass.Bass("TRN2", target_bir_lowering=False)
    S = Sched()
    es = ExitStack()

    def din(name, shape, dt=F32):
        return nc.dram_tensor(name, list(shape), dt, kind="ExternalInput").ap()

    def dout(name, shape, dt=F32):
        return nc.dram_tensor(name, list(shape), dt, kind="ExternalOutput").ap()

    xp = din("xp", [NB, NT * 128, D])
    xs = din("xs", [128, D])
    c18 = din("c18", [18, D])
    ck = din("ck", [NS, 128, 128])
    cv = din("cv", [NS, 128, 128])
    sr = din("sr", [NS, 4, 128, 128])
    wada = din("wada", [D, 3 * D])
    bada = din("bada", [24, 128])
    win = din("win", [D, DIN])
    sinks = din("sinks", [8])
    gnw = din("gnw", [4, 128])
    wout = din("wout", [D, D])
    lnw = din("lnw", [D])
    lnb = din("lnb", [D])
    d_identb = din("identb", [128, 128], BF16)
    d_identf = din("identf", [128, 128])
    d_tabp = din("tabp", [NT, 128, 256])
    d_maskR = din("maskR", [128, 512])
    d_smalls = din("smalls", [128, 36])
    d_cbp = din("cbp", [128, 256], BF16)
    d_cbs = din("cbs", [128, 4224], BF16)
    d_cfs = din("cfs", [128, 768])

    yp = dout("yp", [NB, NT * 128, D])
    ys = dout("ys", [128, D])
    kwp = dout("kwp", [NB, 128, 128])
    vwp = dout("vwp", [NB, 128, 128])
    srp = dout("srp", [NB, 4, 128, 128])
    kws = dout("kws", [NS, 128, 128])
    vws = dout("vws", [NS, 128, 128])
    srs = dout("srs", [NS, 4, 128, 128])

    def sb(name, shape, dt):
        return es.enter_context(nc.sbuf_tensor(name, list(shape), dt))

    win_bf = sb("win_bf", [P, 8, DIN], BF16)
    wout_bf = sb("wout_bf", [P, 8, D], BF16)
    identb = sb("identb_s", [P, 128], BF16)
    identf = sb("identf_s", [P, 128], F32)
    smalls = sb("smalls_s", [P, 36], F32)
    cbp = sb("cbp_s", [P, 2, 128], BF16)
    maskR = sb("maskR_s", [P, 4, 128], F32)
    lnw_bc = sb("lnw_bc", [P, D], F32)
    lnb_bc = sb("lnb_bc", [P, D], F32)
    esink = sb("esink", [P, 8], F32)
    condT = sb("condT", [P, 24, 18], F32)
    scT = sb("scT", [P, 8, 18], BF16)
    bcol = sb("bcol", [P, 24], F32)
    gnwh = sb("gnwh", [P, 512], BF16)
    gate_bc = sb("gate_bc", [P, 3, D], F32)
    gbt = sb("gbt", [P, 2, 128], F32)
    xt = sb("xt", [P, 3, D], F32)
    xb = sb("xb", [P, D], BF16)
    hT = sb("hT", [P, 2, 8, 128], BF16)
    tabt = sb("tabt", [P, 2, 256], F32)
    qa_b = sb("qa_b", [P, 512], BF16)
    kv_f = sb("kv_f", [P, 256], F32)
    ka_b = sb("ka_b", [P, 128], BF16)
    va1 = sb("va1", [P, 2, 2, 65], BF16)
    tg = sb("tg", [P, 512], F32)
    tg2 = sb("tg2", [P, 512], F32)
    sg_a = sb("sg_a", [P, 512], BF16)
    sg_r = sb("sg_r", [P, 512], BF16)
    rA = sb("rA", [P, 512], F32)
    rB = sb("rB", [P, 512], F32)
    rA2 = sb("rA2", [P, 512], F32)
    rB2 = sb("rB2", [P, 512], F32)
    qr_b = sb("qr_b", [P, 4, 128], BF16)
    kr_b = sb("kr_b", [P, 4, 128], BF16)
    ks_b = sb("ks_b", [P, 4, 128], BF16)
    vr_b = sb("vr_b", [P, 4, 128], BF16)
    qaT = sb("qaT", [P, 8, 128], BF16)
    kaT = sb("kaT", [P, 2, 2, 128], BF16)
    qkrT = sb("qkrT", [P, 8, 128], BF16)
    Eb = sb("Eb", [P, 4, 512], BF16)
    Pr = sb("Pr", [P, 4, 128], BF16)
    yn = sb("yn", [P, 4, 128], BF16)
    mixed = sb("mixed", [P, D], BF16)
    mixT = sb("mixT", [P, 8, 128], BF16)
    Sst = sb("Sst", [P, 4, 128], F32)
    Sbf = sb("Sbf", [P, 4, 128], BF16)
    ot = sb("ot", [P, 2, D], F32)
    scr = sb("scr", [P, 64], F32)
    arF = sb("arF", [P, 2816], F32)
    arB = sb("arB", [P, 15008], BF16)
    ps = es.enter_context(nc.psum_tensor("ps", [P, 4096], F32))

    def bank(k, lo=0, hi=512):
        return ps[:, k * 512 + lo:k * 512 + hi]

    def bankb(k):
        return ps[:, k * 512:(k + 1) * 512].bitcast(BF16)

    def PSK(k, lo=0, hi=512):
        return ("ps", k * 512 + lo, k * 512 + hi)

    den = scr[:, 0:8]
    rden = scr[:, 8:16]
    st_gn = scr[:, 16:40]
    mv_gn = scr[:, 40:48]
    vpe = scr[:, 48:52]
    rstd = scr[:, 52:56]
    nbg = scr[:, 56:60]
    scr2 = sb("scr2", [P, 32], F32)
    st_ln = scr2[:, 0:12]
    mv_ln = scr2[:, 12:14]
    vpl = scr2[:, 14:15]
    rsl = scr2[:, 15:16]
    nbl = scr2[:, 16:17]
    esk2 = sb("esk2", [P, 8], F32)

    def DMA(out, in_, reads, writes, sem, q="sp"):
        S.add(q, lambda e: e.dma_start(out=out, in_=in_), reads, writes, dma=sem)

    def MM(out, lhsT, rhs, start, stop, reads, writes):
        S.add("pe", lambda e: e.matmul(out, lhsT, rhs, start=start, stop=stop), reads, writes)

    def TR(out, in_, ident, reads, writes):
        S.add("pe", lambda e: e.transpose(out, in_, ident), reads, writes)

    def ACT(out, in_, func, reads, writes, bias=None, scale=None):
        kw = {}
        if bias is not None:
            kw["bias"] = bias
        if scale is not None:
            kw["scale"] = scale
        S.add("act", lambda e: e.activation(out=out, in_=in_, func=func, **kw), reads, writes)

    def TT(eng, out, in0, in1, op, reads, writes):
        S.add(eng, lambda e: e.tensor_tensor(out, in0, in1, op), reads, writes)

    def TS(eng, out, in0, s1, s2, op0, op1, reads, writes):
        if s2 is None:
            S.add(eng, lambda e: e.tensor_scalar(out, in0, s1, None, op0), reads, writes)
        else:
            S.add(eng, lambda e: e.tensor_scalar(out, in0, s1, s2, op0, op1), reads, writes)

    def STT(out, in0, sc, in1, op0, op1, reads, writes):
        S.add("dve", lambda e: e.scalar_tensor_tensor(out, in0, sc, in1, op0, op1), reads, writes)

    def CP(eng, out, in_, reads, writes):
        if eng == "act":
            S.add("act", lambda e: e.copy(out, in_), reads, writes)
        else:
            S.add(eng, lambda e: e.tensor_copy(out, in_), reads, writes)

    def MEMSET(eng, ap, val, writes):
        S.add(eng, lambda e: e.memset(ap, val), (), writes)

    tapl = []

    def TAP(name, ap, shape, key, dt=F32):
        if taps is None or name not in taps:
            return
        d = dout("tap_" + name, shape, dt)
        DMA(d, ap, [key], [], "tap_" + name)
        tapl.append(name)

    DMA(identb[:], d_identb, [], ["identb"], "c0")
    DMA(identf[:], d_identf, [], ["identf"], "c0")
    DMA(smalls[:], d_smalls, [], ["smalls"], "c0")
    DMA(xt[0:18, 1, :], c18, [], [("xt", 1024, 2048)], "c0")
    DMA(xt[32:56, 1, 0:128], bada, [], [("xtb", 0, 1)], "c0")
    DMA(tg2[:], gnw.rearrange("a b -> (a b)").partition_broadcast(128), [], ["tg2"], "c0")
    DMA(esink[:], sinks.partition_broadcast(128), [], ["esink"], "c0")
    DMA(xt[:, 0, :], xs, [], [("xt", 0, 1024)], "x0")

    kTc_all = arB[:, 4096:8192].rearrange("p (s h c) -> p s h c", s=16, h=2)
    kTcK = ("arB", 4096, 8192)
    va1c_all = arB[:, 8192:10272].rearrange("p (s h d) -> p s h d", s=16, h=2)
    va1cK = ("arB", 8192, 10272)
    ckb = [arB[:, 14496 + s * 256:14496 + (s + 1) * 256].rearrange("p (s d) -> p s d", s=2) for s in range(2)]
    ckbK = [("arB", 14496 + s * 256, 14496 + (s + 1) * 256) for s in range(2)]
    cstg = [xt[:, 2, s * 512:(s + 1) * 512].rearrange("p (kv s d) -> p kv s d", kv=2, s=2) for s in range(2)]
    MEMSET("pool", va1[:, :, :, 64:65], 2.0, [("va1", 0, 2)])
    if do_sample:
        MEMSET("pool", va1c_all[:, :, :, 64:65], 2.0, [va1cK])
        for grp in range(8):
            sl = grp % 2
            kK = ("xt", 2048 + sl * 512, 2048 + sl * 512 + 256)
            vK = ("xt", 2048 + sl * 512 + 256, 2048 + sl * 512 + 512)
            DMA(cstg[sl][:, 0, :, :], ck[2 * grp:2 * grp + 2].rearrange("s c d -> c s d"), [], [kK], "ck%d" % sl)
            DMA(cstg[sl][:, 1, :, :], cv[2 * grp:2 * grp + 2].rearrange("s c d -> c s d"), [], [vK], "cv%d" % sl)
            CP("dve", ckb[sl], cstg[sl][:, 0, :, :], [kK], [ckbK[sl]])
            CP("act", va1c_all[:, 2 * grp:2 * grp + 2, :, 0:64],
               cstg[sl][:, 1, :, :].rearrange("p s (h d) -> p s h d", h=2), [vK], [va1cK])
            t1 = bankb(1)
            for s2 in range(2):
                for h in range(2):
                    q = s2 * 2 + h
                    TR(t1[0:64, q * 128:(q + 1) * 128], ckb[sl][:, s2, h * 64:(h + 1) * 64], identb[:],
                       [ckbK[sl], "identb"], [PSK(1)])
            CP("dve", kTc_all[0:64, 2 * grp:2 * grp + 2, :, :].rearrange("p s h c -> p (s h c)"), t1[0:64, 0:512],
               [PSK(1)], [kTcK])
        DMA(arB[:, 10272:14496], d_cbs, [], [("arB", 10272, 14496)], "c1")
        DMA(tabt[:, 0, :], d_cfs[:, 512:768], [], [("tabt", 0, 1)], "c1")
    DMA(cbp[:].rearrange("p a b -> p (a b)"), d_cbp, [], ["cbp"], "c1")
    DMA(maskR[:].rearrange("p a b -> p (a b)"), d_maskR, [], ["maskR"], "c1")
    DMA(lnw_bc[:], lnw.partition_broadcast(128), [], ["lnw_bc"], "c1")
    DMA(lnb_bc[:], lnb.partition_broadcast(128), [], ["lnb_bc"], "c1")

    TS("dve", gnwh[:], tg2[:], 0.5, None, ALU.mult, None, ["tg2"], ["gnwh"])
    ACT(esink[:], esink[:], AF.Exp, ["esink"], ["esink"], bias=float(math.log(2.0)))
    for hf, tb_ in enumerate((tg, tg2)):
        cs = xt[0:18, 1, hf * 512:(hf + 1) * 512]
        ACT(tb_[0:18, :], cs, AF.Tanh, [("xt", 1024, 2048)], ["tg" if hf == 0 else "tg2"], scale=0.5)
        STT(cs, tb_[0:18, :], 1.0, cs, ALU.add, ALU.mult, ["tg" if hf == 0 else "tg2", ("xt", 1024, 2048)],
            [("xt", 1024, 2048)])
    for kc in range(8):
        TR(bank(0, kc * 18, kc * 18 + 18), xt[0:18, 1, kc * 128:(kc + 1) * 128], identf[0:18, 0:18],
           [("xt", 1024, 2048), "identf"], [PSK(0)])
    ACT(scT[:].rearrange("p a b -> p (a b)"), bank(0, 0, 144), AF.Identity, [PSK(0)], ["scT"], scale=0.5)
    TR(bank(0, 160, 184), xt[32:56, 1, 0:128], identf[32:56, 32:56], [("xtb", 0, 1), "identf"], [PSK(0)])
    CP("dve", bcol[:], bank(0, 160, 184), [PSK(0)], ["bcol"])

    wada_v = wada.rearrange("(kc p) n -> p kc n", p=128)
    win_v = win.rearrange("(kc p) n -> p kc n", p=128)
    wout_v = wout.rearrange("(kc p) n -> p kc n", p=128)
    grp_cols = [(C_QR, 512), (C_KR, 512), (C_QA, 512), (C_K, 256), (C_VR, 512), (C_GA, 512), (C_GR, 512)]
    win_chunks = [cc for (c0, w) in grp_cols for cc in range(c0, c0 + w, 256)]
    for ch in range(12):
        pos = win_chunks[ch]
        DMA(win_bf[:, :, pos:pos + 256], wada_v[:, :, ch * 256:(ch + 1) * 256], [], [("win", pos, pos + 256)],
            "wa%d" % ch, q="pool")
    cond_banks = [3, 4, 5]
    for ch in range(12):
        pos = win_chunks[ch]
        bk = cond_banks[ch % 3]
        for jj in range(2):
            for kc in range(8):
                MM(bank(bk, jj * 18, jj * 18 + 18), win_bf[:, kc, pos + jj * 128:pos + (jj + 1) * 128], scT[:, kc, :],
                   kc == 0, kc == 7, [("win", pos, pos + 256), "scT"], [PSK(bk)])
        for jj in range(2):
            j = ch * 2 + jj
            TS("dve", condT[:, j, :], bank(bk, jj * 18, jj * 18 + 18), bcol[:, j:j + 1], None, ALU.add, None,
               [PSK(bk), "bcol"], ["condT"])
    TS("dve", condT[:, 8:16, :], condT[:, 8:16, :], 1.0, None, ALU.add, None, ["condT"], ["condT"])
    for k, cc in enumerate(win_chunks):
        DMA(win_bf[:, :, cc:cc + 256], win_v[:, :, cc:cc + 256], [], [("win", cc, cc + 256)], "wi%d" % k, q="pool")
    for c in range(4):
        cc = c * 256
        DMA(wout_bf[:, :, cc:cc + 256], wout_v[:, :, cc:cc + 256], [], [("wout", cc, cc + 256)], "wo%d" % c, q="pool")

    def gate_tile(dst, dstK, src_fn):
        for j in range(8):
            sl = j % 2
            src = src_fn(j)
            dstv = gbt[:, sl, :] if len(src.shape) == 2 else gbt[:, sl, :].rearrange("p (s i) -> p s i", s=16)
            CP("dve", dstv, src, ["condT"], [("gbt", sl * 128, sl * 128 + 128)])
            TR(ps[:, 512 + j * 128:512 + (j + 1) * 128], gbt[:, sl, :], identf[:],
               [("gbt", sl * 128, sl * 128 + 128), "identf"], [PSK(1 + j // 4)])
        CP("act", dst, ps[:, 512:1536], [PSK(1), PSK(2)], [dstK])

    for b in range(NB):
        gate_tile(gate_bc[:, b, :], ("gate_bc", b * 1024, (b + 1) * 1024),
                  lambda j, b=b: condT[:, 16 + j, b:b + 1].broadcast_to([P, 128]))

    if do_sample:
        gate_tile(gate_bc[:, 2, :], ("gate_bc", 2048, 3072),
                  lambda j: condT[:, 16 + j, 2:18].unsqueeze(2).broadcast_to([P, 16, 8]))

    pending_hooks = []

    def win_loop():
        for f in pending_hooks:
            f()

    def xK_(slot):
        return ("xt", slot * 1024, (slot + 1) * 1024)

    def hK_(slot):
        return ("hT", slot * 1024, (slot + 1) * 1024)

    def f_cast(xslot):
        CP("act", xb[:], xt[:, xslot, :], [xK_(xslot)], ["xb"])

    def f_tr(bk=0):
        t0b = bankb(bk)
        for c in range(8):
            TR(t0b[:, c * 128:(c + 1) * 128], xb[:, c * 128:(c + 1) * 128], identb[:], ["xb", "identb"], [PSK(bk)])

    def f_hevac(hslot, mode, b, bk=0, on_dve=False):
        t0b = bankb(bk)
        hK = hK_(hslot)
        if mode == "p":
            for c in range(8):
                hKc = hK
                if not on_dve:
                    ACT(hT[:, hslot, c, :], t0b[:, c * 128:(c + 1) * 128], AF.Identity, [PSK(bk), "condT"], [hKc],
                        bias=condT[:, c, b:b + 1], scale=condT[:, 8 + c, b:b + 1])
                else:
                    TS("dve", hT[:, hslot, c, :], t0b[:, c * 128:(c + 1) * 128], condT[:, 8 + c, b:b + 1],
                       condT[:, c, b:b + 1], ALU.mult, ALU.add, [PSK(bk), "condT"], [hKc])
        else:
            tmp = ot[:, 1, :].rearrange("p (c s i) -> p c s i", c=8, s=16)
            sc1e = condT[:, 8:16, 2:18].unsqueeze(3).broadcast_to([P, 8, 16, 8])
            she = condT[:, 0:8, 2:18].unsqueeze(3).broadcast_to([P, 8, 16, 8])
            TT("dve", tmp, t0b.rearrange("p (c s i) -> p c s i", c=8, s=16), sc1e, ALU.mult,
               [PSK(bk), "condT"], [("ot", 1024, 2048)])
            TT("dve", hT[:, hslot, :, :].rearrange("p c (s i) -> p c s i", s=16), tmp, she, ALU.add,
               [("ot", 1024, 2048), "condT"], [hK])

    def zgroup(hslot, bk, c0, w):
        for kc in range(8):
            MM(bank(bk, 0, w), hT[:, hslot, kc, :], win_bf[:, kc, c0:c0 + w], kc == 0, kc == 7,
               [hK_(hslot), ("win", c0, c0 + w)], [PSK(bk)])

    def rotary(bk, dst, dstK, tab_ap, tabK, rA=rA, rB=rB, nmA="rA", nmB="rB"):
        cc_b = tab_ap[:, 0:128].unsqueeze(1).broadcast_to([P, 4, 128])
        ss0 = tab_ap[:, 128:192].unsqueeze(1).broadcast_to([P, 4, 64])
        ss1 = tab_ap[:, 192:256].unsqueeze(1).broadcast_to([P, 4, 64])
        z4 = bank(bk).rearrange("p (h t d) -> p h t d", h=4, t=2)
        TT("dve", rA[:].rearrange("p (h d) -> p h d", h=4), bank(bk).rearrange("p (h d) -> p h d", h=4), cc_b,
           ALU.mult, [PSK(bk), tabK], [nmA])
        rB4 = rB[:].rearrange("p (h t d) -> p h t d", h=4, t=2)
        TT("dve", rB4[:, :, 0, :], z4[:, :, 1, :], ss0, ALU.mult, [PSK(bk), tabK], [(nmB, 0, 1)])
        TT("dve", rB4[:, :, 1, :], z4[:, :, 0, :], ss1, ALU.mult, [PSK(bk), tabK], [(nmB, 1, 2)])
        TT("pool", dst[:].rearrange("p h d -> p (h d)"), rA[:], rB[:], ALU.add, [nmA, nmB], [dstK])

    def z_qr(hslot, bk, tab_ap, tabK):
        zgroup(hslot, bk, C_QR, 512)
        rotary(bk, qr_b, "qr_b", tab_ap, tabK)

    def z_kr(hslot, bk, tab_ap, tabK, kdec_ap):
        zgroup(hslot, bk, C_KR, 512)
        rotary(bk, kr_b, "kr_b", tab_ap, tabK, rA2, rB2, "rA2", "rB2")
        TT("pool", ks_b[:], kr_b[:], kdec_ap.unsqueeze(2).broadcast_to([P, 4, 128]), ALU.mult,
           ["kr_b", "smalls"], ["ks_b"])

    def z_qa(hslot, bk):
        zgroup(hslot, bk, C_QA, 512)
        CP("act", qa_b[:], bank(bk), [PSK(bk)], ["qa_b"])

    def z_kv(hslot, bk, vslot, want_f32):
        zgroup(hslot, bk, C_K, 256)
        CP("act", ka_b[:], bank(bk, 0, 128), [PSK(bk)], ["ka_b"])
        CP("act", va1[:, vslot, :, 0:64], bank(bk, 128, 256).rearrange("p (h d) -> p h d", h=2), [PSK(bk)],
           [("va1", vslot, vslot + 1)])
        if want_f32:
            CP("act", kv_f[:], bank(bk, 0, 256), [PSK(bk)], ["kv_f"])

    def z_vr(hslot, bk):
        zgroup(hslot, bk, C_VR, 512)
        CP("act", vr_b[:].rearrange("p h d -> p (h d)"), bank(bk), [PSK(bk)], ["vr_b"])

    def z_ga(hslot, bk):
        zgroup(hslot, bk, C_GA, 512)
        ACT(tg[:], bank(bk), AF.Tanh, [PSK(bk)], ["tg"], scale=0.5)
        STT(sg_a[:], tg[:], 1.0, bank(bk), ALU.add, ALU.mult, ["tg", PSK(bk)], ["sg_a"])

    def z_gr(hslot, bk):
        zgroup(hslot, bk, C_GR, 512)
        ACT(tg2[:], bank(bk), AF.Tanh, [PSK(bk)], ["tg2"], scale=0.5)
        STT(sg_r[:], tg2[:], 1.0, bank(bk), ALU.add, ALU.mult, ["tg2", PSK(bk)], ["sg_r"])

    def m_tr(kslot, b1, b2):
        m_tr_banks(kslot, b1, b2, 0)

    def m_tr_banks(kslot, b1, b2, b0):
        t2 = bankb(b2)
        for h in range(4):
            TR(t2[:, h * 128:(h + 1) * 128], qr_b[:, h, :], identb[:], ["qr_b", "identb"], [PSK(b2)])
        for h in range(4):
            TR(t2[:, (4 + h) * 128:(5 + h) * 128], kr_b[:, h, :], identb[:], ["kr_b", "identb"], [PSK(b2)])
        t1 = bankb(b1)
        for h in range(8):
            TR(t1[0:64, h * 128:(h + 1) * 128], qa_b[:, h * 64:(h + 1) * 64], identb[:], ["qa_b", "identb"], [PSK(b1)])
        t0 = bankb(b0)
        for h in range(2):
            TR(t0[0:64, h * 128:(h + 1) * 128], ka_b[:, h * 64:(h + 1) * 64], identb[:], ["ka_b", "identb"], [PSK(b0)])
        CP("act", qaT[0:64, :, :].rearrange("p a b -> p (a b)"), t1[0:64, :], [PSK(b1)], ["qaT"])
        CP("act", kaT[0:64, kslot, :, :].rearrange("p a b -> p (a b)"), t0[0:64, 0:256], [PSK(b0)],
           [("kaT", kslot, kslot + 1)])
        CP("dve", qkrT[:].rearrange("p a b -> p (a b)"), t2, [PSK(b2)], ["qkrT"])

    def score(bk, lhsT, lhsK, h, neg_ap, negK, es_):
        MM(bank(bk), lhsT, qaT[0:64, 4 * h:4 * h + 4, :].rearrange("p a b -> p (a b)"), True, False,
           [lhsK, "qaT"], [PSK(bk)])
        MM(bank(bk), identb[:], neg_ap.unsqueeze(1).broadcast_to([P, 4, 128]), False, True,
           ["identb", negK], [PSK(bk)])
        ACT(Eb[:, es_, :], bank(bk), AF.Exp, [PSK(bk)], [("Eb", es_, es_ + 1)], scale=0.125)

    def attn_finish(o_ap_fn, oK_fn, heads, den_ap=None):
        h0, h1 = heads[0], heads[-1] + 1
        if den_ap is not None:
            TT("dve", den[:, h0:h1], den_ap, esink[:, h0:h1], ALU.add, [oK_fn(h0), "esink"], [("scr", h0, h1)])
        else:
            for hd in heads:
                TT("dve", den[:, hd:hd + 1], o_ap_fn(hd)[:, 64:65], esink[:, hd:hd + 1], ALU.add,
                   [oK_fn(hd), "esink"], [("scr", hd, hd + 1)])
        S.add("dve", lambda e: e.reciprocal(rden[:, h0:h1], den[:, h0:h1]),
              [("scr", h0, h1)], [("scr", 8 + h0, 8 + h1)])
        for hd in heads:
            STT(mixed[:, hd * 64:(hd + 1) * 64], o_ap_fn(hd)[:, 0:64], rden[:, hd:hd + 1],
                sg_a[:, hd * 64:(hd + 1) * 64], ALU.mult, ALU.mult,
                [oK_fn(hd), ("scr", 8 + hd, 9 + hd), "sg_a"], [("mixed", hd * 64, (hd + 1) * 64)])

    def gn_a(u_ap_fn, uK_fn, epsp_ap):
        for h in range(4):
            S.add("dve", lambda e, h=h: e.bn_stats(st_gn[:, h * 6:(h + 1) * 6], u_ap_fn(h)),
                  [uK_fn(h)], [("scr", 16 + h * 6, 22 + h * 6)])
            S.add("dve", lambda e, h=h: e.bn_aggr(mv_gn[:, h * 2:(h + 1) * 2], st_gn[:, h * 6:(h + 1) * 6]),
                  [("scr", 16 + h * 6, 22 + h * 6)], [("scr", 40 + 2 * h, 42 + 2 * h)])
        mv3 = mv_gn.rearrange("p (h t) -> p h t", t=2)
        TT("dve", vpe, mv3[:, :, 1], epsp_ap, ALU.add, [("scr", 40, 48), "smalls"], [("scr", 48, 52)])
        TT("pool", rstd, vpe, smalls[:, 16:20], ALU.pow, [("scr", 48, 52), "smalls"], [("scr", 52, 56)])

    def gn_b(u_ap_fn, uK_fn):
        mv3 = mv_gn.rearrange("p (h t) -> p h t", t=2)
        STT(nbg, mv3[:, :, 0], -1.0, rstd, ALU.mult, ALU.mult, [("scr", 40, 48), ("scr", 52, 56)], [("scr", 56, 60)])
        for h in range(4):
            ACT(yn[:, h, :], u_ap_fn(h), AF.Identity, [uK_fn(h), ("scr", 52, 60)], [("yn", h, h + 1)],
                bias=nbg[:, h:h + 1], scale=rstd[:, h:h + 1])

    def gn_c():
        TT("pool", yn[:].rearrange("p h d -> p (h d)"), yn[:].rearrange("p h d -> p (h d)"), gnwh[:], ALU.mult,
           ["yn", "gnwh"], ["yn"])
        TT("pool", mixed[:, 512:1024], yn[:].rearrange("p h d -> p (h d)"), sg_r[:], ALU.mult,
           ["yn", "sg_r"], [("mixed", 512, 1024)])

    def b_tr():
        t0b = bankb(0)
        for c in range(8):
            TR(t0b[:, c * 128:(c + 1) * 128], mixed[:, c * 128:(c + 1) * 128], identb[:], ["mixed", "identb"], [PSK(0)])
        CP("act", mixT[:].rearrange("p a b -> p (a b)"), t0b, [PSK(0)], ["mixT"])

    def b_proj():
        for nh in range(2):
            for kc in range(8):
                MM(bank(1 + nh), mixT[:, kc, :], wout_bf[:, kc, nh * 512:(nh + 1) * 512], kc == 0, kc == 7,
                   ["mixT", ("wout", nh * 512, nh * 512 + 512)], [PSK(1 + nh)])

    def b_ln_a(xslot, oslot, gate_ap, gateK, defer_r=False):
        oK = ("ot", oslot * 1024, (oslot + 1) * 1024)
        o = ot[:, oslot, :]
        TT("dve", o, ps[:, 512:1536], gate_ap, ALU.mult, [PSK(1), PSK(2), gateK], [oK])
        if not defer_r:
            STT(o, xt[:, xslot, :], ALPHA, o, ALU.mult, ALU.add, [xK_(xslot), oK], [oK])

    def b_ln_r(xslot, oslot):
        oK = ("ot", oslot * 1024, (oslot + 1) * 1024)
        o = ot[:, oslot, :]
        STT(o, xt[:, xslot, :], ALPHA, o, ALU.mult, ALU.add, [xK_(xslot), oK], [oK])

    def b_ln_a2(oslot):
        oK = ("ot", oslot * 1024, (oslot + 1) * 1024)
        o = ot[:, oslot, :]
        for k in range(2):
            S.add("dve", lambda e, k=k: e.bn_stats(st_ln[:, k * 6:(k + 1) * 6], o[:, k * 512:(k + 1) * 512]),
                  [oK], [("scr2", k * 6, k * 6 + 6)])
        S.add("dve", lambda e: e.bn_aggr(mv_ln, st_ln), [("scr2", 0, 12)], [("scr2", 12, 14)])
        TS("dve", vpl, mv_ln[:, 1:2], LN_EPS, None, ALU.add, None, [("scr2", 12, 14)], [("scr2", 14, 15)])
        TT("pool", rsl, vpl, smalls[:, 16:17], ALU.pow, [("scr2", 14, 15), "smalls"], [("scr2", 15, 16)])

    def b_ln_b(oslot, out_dram, osem):
        oK = ("ot", oslot * 1024, (oslot + 1) * 1024)
        o = ot[:, oslot, :]
        STT(nbl, mv_ln[:, 0:1], -1.0, rsl, ALU.mult, ALU.mult, [("scr2", 12, 14), ("scr2", 15, 16)], [("scr2", 16, 17)])
        ACT(o, o, AF.Identity, [oK, ("scr2", 15, 17)], [oK], bias=nbl, scale=rsl)
        TT("pool", o, o, lnw_bc[:], ALU.mult, [oK, "lnw_bc"], [oK])
        TT("pool", o, o, lnb_bc[:], ALU.add, [oK, "lnb_bc"], [oK])
        DMA(out_dram, o, [oK], [], osem)

    def sample_phase(hook_early=None, hook_mid=None, hook_late=None):
        NSL = 7
        Ssl = [arF[:, s * 256:(s + 1) * 256].rearrange("p (h e) -> p h e", h=2) for s in range(NSL)]
        SslK = [("arF", s * 256, (s + 1) * 256) for s in range(NSL)]
        Snw = [arF[:, 1792 + s * 256:1792 + (s + 1) * 256].rearrange("p (h e) -> p h e", h=2) for s in range(2)]
        SnwK = [("arF", 1792 + s * 256, 1792 + (s + 1) * 256) for s in range(2)]
        maskRs = arF[:, 2304:2816]
        gate_s = gate_bc[:, 2, :]
        Sbs = [arB[:, s * 256:(s + 1) * 256].rearrange("p (h e) -> p h e", h=2) for s in range(3)]
        SbsK = [("arB", s * 256, (s + 1) * 256) for s in range(3)]
        ksx = [arB[:, 768 + s * 256:768 + (s + 1) * 256].rearrange("p (h d) -> p h d", h=2) for s in range(2)]
        ksxK = [("arB", 768 + s * 256, 768 + (s + 1) * 256) for s in range(2)]
        qsx = arB[:, 4096:8192].rearrange("p (h s t) -> p h s t", h=2, s=16)
        qsxK = ("arB", 4096, 8192)
        Mn = arB[:, 10272:10400]
        Mc = arB[:, 10400:12448].rearrange("p (s t) -> p s t", s=16)
        selT = arB[:, 12448:14496].rearrange("p (s t) -> p s t", s=16)
        cbsK = ("arB", 10272, 14496)

        DMA(arF[:, 2304:2816], d_cfs[:, 0:512], [], [("arF", 2304, 2816)], "c2")
        DMA(kws[:, 0:120, :], ck[:, 8:128, :], [], [], "wo")
        DMA(vws[:, 0:120, :], cv[:, 8:128, :], [], [], "wo")

        m_tr(0, 1, 2)
        for s in range(NS):
            DMA(kws[s, 120:128, :], kv_f[8 * s:8 * s + 8, 0:128], ["kv_f"], [], "wo")
            DMA(vws[s, 120:128, :], kv_f[8 * s:8 * s + 8, 128:256], ["kv_f"], [], "wo")

        obanks = [5, 6, 7, 2]
        sbanks = [3, 4]
        ecnt = [0]

        def nxt():
            i = ecnt[0]
            ecnt[0] += 1
            return sbanks[i % 2], i % 4

        for h in range(2):
            def do_score(k):
                bk, es_ = nxt()
                if k == 0:
                    score(bk, kaT[0:64, 0, h, :], ("kaT", 0, 1), h, Mn, cbsK, es_)
                else:
                    sq = k - 1
                    eK = ("Eb", es_, es_ + 1)
                    MEMSET("pool", Eb[:, es_, :], 0.0, [eK])
                    sc = bank(bk, 0, 32)
                    MM(sc, kTc_all[0:64, sq, h, :], qaT[0:64, 4 * h:4 * h + 4, 8 * sq:8 * sq + 8], True, False,
                       [kTcK, "qaT"], [PSK(bk)])
                    MM(sc, identb[:], Mc[:, sq, 8 * sq:8 * sq + 8].unsqueeze(1).broadcast_to([P, 4, 8]), False, True,
                       ["identb", cbsK], [PSK(bk)])
                    ACT(Eb[:, es_, :].rearrange("p (g t) -> p g t", g=4)[:, :, 8 * sq:8 * sq + 8],
                        sc.rearrange("p (g i) -> p g i", g=4), AF.Exp, [PSK(bk)], [eK], scale=0.125)
                return es_

            def do_pv(k, es_):
                vap, vK = (va1[:, 0, h, :], ("va1", 0, 1)) if k == 0 else (va1c_all[:, k - 1, h, :], va1cK)
                for g in range(4):
                    MM(bank(obanks[g], 0, 65), Eb[:, es_, g * 128:(g + 1) * 128], vap, k == 0, k == NS,
                       [("Eb", es_, es_ + 1), vK], [PSK(obanks[g])])

            pend_e = [do_score(0), do_score(1)]
            for k in range(NS + 1):
                if k + 2 <= NS:
                    pend_e.append(do_score(k + 2))
                do_pv(k, pend_e[k])
            attn_finish(lambda hd: bank(obanks[hd % 4], 0, 65), lambda hd: PSK(obanks[hd % 4]),
                        [4 * h + g for g in range(4)])

        if hook_early is not None:
            hook_early()
        for hh in range(4):
            MM(bank(1, hh * 128, (hh + 1) * 128), qkrT[:, 4 + hh, :], qkrT[:, hh, :], True, True, ["qkrT"], [PSK(1)])
        TT("dve", Pr[:].rearrange("p h d -> p (h d)"), bank(1), maskRs, ALU.mult, [PSK(1), ("arF", 2304, 2816)], ["Pr"])
        ubanks = [5, 6, 7, 2]
        dsbanks = [3, 4]
        it = 0
        for hp in range(2):
            for hh in range(2):
                TT("dve", qsx[:, hh, :, :], qkrT[:, 2 * hp + hh, :].unsqueeze(1).broadcast_to([P, 16, 128]), selT,
                   ALU.mult, ["qkrT", cbsK], [qsxK])
            for hh in range(2):
                hd = 2 * hp + hh
                MM(bank(ubanks[hd], 0, 128), Pr[:, hd, :], vr_b[:, hd, :], True, False, ["Pr", "vr_b"], [PSK(ubanks[hd])])
            for s in range(NS):
                s3 = it % NSL
                sb3 = it % 3
                s2 = it % 2
                PF = NSL - 1
                if it == 0:
                    for q in range(PF):
                        DMA(Ssl[q], sr[q, 0:2].rearrange("h d e -> d h e"), [], [SslK[q]], "ss%d" % q)
                nx = it + PF
                if nx < 2 * NS:
                    hpn, sn = nx // NS, nx % NS
                    DMA(Ssl[nx % NSL], sr[sn, 2 * hpn:2 * hpn + 2].rearrange("h d e -> d h e"), [], [SslK[nx % NSL]],
                        "ss%d" % (nx % NSL))
                it += 1
                CP("act", Sbs[sb3], Ssl[s3], [SslK[s3]], [SbsK[sb3]])
                ACT(ksx[s2], ks_b[:, 2 * hp:2 * hp + 2, :], AF.Identity, ["ks_b", "smalls"], [ksxK[s2]],
                    scale=smalls[:, 20 + s:21 + s])
                for hh in range(2):
                    hd = 2 * hp + hh
                    MM(bank(ubanks[hd], 0, 128), qsx[:, hh, s, :], Sbs[sb3][:, hh, :], False, s == NS - 1,
                       [qsxK, SbsK[sb3]], [PSK(ubanks[hd])])
                db = dsbanks[s2]
                for hh in range(2):
                    hd = 2 * hp + hh
                    MM(bank(db, hh * 128, (hh + 1) * 128), ksx[s2][:, hh, :], vr_b[:, hd, :], True, True,
                       [ksxK[s2], "vr_b"], [PSK(db)])
                for hh in range(2):
                    hd = 2 * hp + hh
                    STT(Snw[s2][:, hh, :], Ssl[s3][:, hh, :], float(GAM[hd] ** 8), bank(db, hh * 128, (hh + 1) * 128),
                        ALU.mult, ALU.add, [SslK[s3], PSK(db)], [SnwK[s2]])
                DMA(srs[s, 2 * hp:2 * hp + 2].rearrange("h d e -> d h e"), Snw[s2], [SnwK[s2]], [], "so%d" % s2)
        ufn = lambda h: bank(ubanks[h], 0, 128)
        uKf = lambda h: PSK(ubanks[h])
        gn_a(ufn, uKf, smalls[:, 8:12])
        gn_b(ufn, uKf)
        gn_c()
        if hook_mid is not None:
            hook_mid()
        b_tr()
        b_proj()
        if hook_late is not None:
            hook_late()
        b_ln_a(0, 0, gate_s, ("gate_bc", 2048, 3072))
        b_ln_a2(0)
        b_ln_b(0, ys, "yo0")

    if do_sample and not phase0_only:
        tabs0 = tabt[:, 0, :]
        tabs0K = ("tabt", 0, 1)
        f_cast(0)
        f_tr()
        f_hevac(0, "s", None)
        pending_hooks.extend([
            lambda: z_qr(0, 7, tabs0, tabs0K),
            lambda: z_kr(0, 3, tabs0, tabs0K, smalls[:, 12:16]),
            lambda: z_qa(0, 4),
            lambda: z_kv(0, 5, 0, True),
            lambda: z_vr(0, 6),
            lambda: z_ga(0, 7),
            lambda: z_gr(0, 3),
        ])
    win_loop()
    DMA(xt[:, 1, :], xp[0, 0:128, :], [], [("xt", 1024, 2048), ("xtb", 0, 2)], "x1")

    tiles = [(b, n) for b in range(NB) for n in range(NT)]
    NTILES = len(tiles) if ntiles is None else ntiles
    if phase0_only:
        NTILES = 0

    def xslot_(i):
        return (i + 1) % 3

    def load_x(i):
        b, n = tiles[i]
        sl = xslot_(i)
        DMA(xt[:, sl, :], xp[b, n * 128:(n + 1) * 128, :], [], [xK_(sl)], "x%d" % sl)

    def load_tab(i):
        b, n = tiles[i]
        sl = i % 2
        DMA(tabt[:, sl, :], d_tabp[n], [], [("tabt", sl, sl + 1)], "tb%d" % sl)

    def tabK_(i):
        return ("tabt", i % 2, i % 2 + 1)

    def fa(i):
        b, n = tiles[i]
        f_cast(xslot_(i))
        f_tr()
        f_hevac(i % 2, "p", b)

    SB = {(0, "c"): 6, (0, "p"): 7, (1, "c"): 4, (1, "p"): 5}
    ES = {(0, "c"): 0, (0, "p"): 1, (1, "c"): 2, (1, "p"): 3}
    OB = [6, 4]

    def o_ap(hd):
        return bank(OB[hd // 4], (hd % 4) * 65, (hd % 4) * 65 + 65)

    def o_K(hd):
        return PSK(OB[hd // 4])

    def u_ap(h):
        return bank(2, h * 128, (h + 1) * 128)

    def u_K(h):
        return PSK(2)

    pending_ln = [None]
    def z_first(j, hs):
        z_qr(hs, 7, tabt[:, j % 2, :], tabK_(j))
        z_kr(hs, 5, tabt[:, j % 2, :], tabK_(j), smalls[:, 4:8])

    def z_all(j, hs):
        bj, nj = tiles[j]
        z_qa(hs, 1)
        z_kv(hs, 0, j % 2, nj == NT - 1)
        z_vr(hs, 3)

    def z_rest(hs):
        z_ga(hs, 7)
        z_gr(hs, 5)

    def m_tr2(kslot):
        m_tr_banks(kslot, 4, 6, 3)

    def iteration(i):
        S.cur = i
        b, n = tiles[i]
        ks, kp = i % 2, (i + 1) % 2
        hs_next = (i + 1) % 2
        has_next = i + 1 < NTILES
        kinds = ["c", "p"] if n > 0 else ["c"]
        pend = pending_ln[0]
        pending_ln[0] = None
        if pend is not None:
            pend[0]()
        if i + 2 < NTILES:
            load_tab(i + 2)
        if n == NT - 1:
            DMA(kwp[b], kv_f[:, 0:128], ["kv_f"], [], "wk")
            DMA(vwp[b], kv_f[:, 128:256], ["kv_f"], [], "wk")
        for hh in range(4):
            MM(bank(3, hh * 128, (hh + 1) * 128), ks_b[:, hh, :], vr_b[:, hh, :], True, True, ["ks_b", "vr_b"], [PSK(3)])
        for hh in range(4):
            MM(bank(0, hh * 128, (hh + 1) * 128), qkrT[:, 4 + hh, :], qkrT[:, hh, :], True, True, ["qkrT"], [PSK(0)])
        TT("dve", Pr[:].rearrange("p h d -> p (h d)"), bank(0), maskR[:].rearrange("p h d -> p (h d)"), ALU.mult,
           [PSK(0), "maskR"], ["Pr"])
        if n == 0:
            CP("dve", Sst[:].rearrange("p h e -> p (h e)"), bank(3), [PSK(3)], ["Sst"])
        else:
            for hh in range(4):
                STT(Sst[:, hh, :], Sst[:, hh, :], float(GAM[hh] ** 128), bank(3, hh * 128, (hh + 1) * 128),
                    ALU.mult, ALU.add, ["Sst", PSK(3)], ["Sst"])
        if n == NT - 1:
            DMA(srp[b].rearrange("h d e -> d h e"), Sst[:], ["Sst"], [], "wS")
        for h in range(2):
            for kind in kinds:
                slot = ks if kind == "c" else kp
                neg = cbp[:, 1, :] if kind == "c" else cbp[:, 0, :]
                score(SB[(h, kind)], kaT[0:64, slot, h, :], ("kaT", slot, slot + 1), h, neg, "cbp", ES[(h, kind)])
        if pend is not None:
            pend[1]()
        if i + 2 < NTILES:
            load_x(i + 2)
        for hh in range(4):
            if n > 0:
                MM(u_ap(hh), qkrT[:, hh, :], Sbf[:, hh, :], True, False, ["qkrT", "Sbf"], [PSK(2)])
            MM(u_ap(hh), Pr[:, hh, :], vr_b[:, hh, :], n == 0, True, ["Pr", "vr_b"], [PSK(2)])
        if n != NT - 1:
            CP("act", Sbf[:].rearrange("p h e -> p (h e)"), Sst[:].rearrange("p h e -> p (h e)"), ["Sst"], ["Sbf"])
        gn_a(u_ap, u_K, smalls[:, 0:4])
        if has_next:
            z_qr(hs_next, 7, tabt[:, (i + 1) % 2, :], tabK_(i + 1))
        gn_b(u_ap, u_K)
        gn_c()
        if has_next:
            z_kr(hs_next, 5, tabt[:, (i + 1) % 2, :], tabK_(i + 1), smalls[:, 4:8])
        for h in range(2):
            for g in range(4):
                o = o_ap(4 * h + g)
                e_ = ES[(h, "c")]
                MM(o, Eb[:, e_, g * 128:(g + 1) * 128], va1[:, ks, h, :], True, n == 0,
                   [("Eb", e_, e_ + 1), ("va1", ks, ks + 1)], [PSK(OB[h])])
                if n > 0:
                    e_ = ES[(h, "p")]
                    MM(o, Eb[:, e_, g * 128:(g + 1) * 128], va1[:, kp, h, :], False, True,
                       [("Eb", e_, e_ + 1), ("va1", kp, kp + 1)], [PSK(OB[h])])
        for h in range(2):
            dview = bank(OB[h], 0, 260).rearrange("p (g c) -> p g c", g=4)[:, :, 64]
            attn_finish(o_ap, o_K, [4 * h + g for g in range(4)], dview)
        if has_next:
            z_all(i + 1, hs_next)
        if pend is not None:
            pend[2]()
        if has_next:
            z_rest(hs_next)
        osl = (i + 1) % 2
        if i + 2 < NTILES:
            f_cast(xslot_(i + 2))
        b_tr()
        if has_next:
            m_tr2((i + 1) % 2)
        if i + 2 < NTILES:
            f_tr(7)
            f_hevac(i % 2, "p", tiles[i + 2][0], 7, True)
        b_proj()
        xs_i = xslot_(i)
        pending_ln[0] = (
            lambda: b_ln_a(xs_i, osl, gate_bc[:, b, :], ("gate_bc", b * 1024, (b + 1) * 1024), True),
            lambda: (b_ln_r(xs_i, osl), b_ln_a2(osl)),
            lambda: b_ln_b(osl, yp[b, n * 128:(n + 1) * 128, :], "yo%d" % osl),
        )

    def pro_early():
        load_tab(0)
        if NTILES > 1:
            load_x(1)
            load_tab(1)
        fa(0)

    def pro_mid():
        tb, tk = tabt[:, 0, :], tabK_(0)
        z_qr(0, 0, tb, tk)
        z_kr(0, 1, tb, tk, smalls[:, 4:8])
        z_qa(0, 3)
        z_kv(0, 4, 0, False)
        z_vr(0, 3)

    def pro_late():
        z_ga(0, 0)
        z_gr(0, 3)
        m_tr2(0)
        if NTILES > 1:
            f_cast(xslot_(1))
            f_tr(7)
            f_hevac(1, "p", tiles[1][0], 7)

    if do_sample and not phase0_only:
        if NTILES > 0:
            sample_phase(pro_early, pro_mid, pro_late)
        else:
            sample_phase()
    elif NTILES > 0:
        pro_early()
        pro_mid()
        pro_late()
    for i in range(NTILES):
        iteration(i)
    if pending_ln[0] is not None:
        for f in pending_ln[0]:
            f()
    sem_names = ["pe", "act", "dve", "pool"] + sorted(k for k in S.cnt if k.startswith("D:"))
    sems = {}
    for nm in sem_names:
        sems[nm] = es.enter_context(nc.semaphore("s_" + nm.replace(":", "_")))
    GROUP = {"D:c0", "D:c1", "D:c2"}
    final_waits = [(k, v) for k, v in S.cnt.items() if k.startswith("D:")]

    def emit(name, e, tail=False):
        issued = {}
        for fn, waits, pid, inc, tag in S.ops[name]:
            for p, v in waits:
                if p in GROUP:
                    assert name != "sp" or issued.get(p, 0) == S.cnt[p], (p, v)
                    v = S.cnt[p]
                e.wait_ge(sems[p], v)
            ins = fn(e)
            ins.then_inc(sems[pid], inc)
            if TAGS is not None:
                try:
                    TAGS[ins.ins.name] = tag
                except Exception:
                    pass
            issued[pid] = issued.get(pid, 0) + inc
        if tail:
            for p, v in final_waits:
                e.wait_ge(sems[p], v)

    with nc.Block() as block:
        @block.tensor
        def _(e):
            emit("pe", e)

        @block.scalar
        def _(e):
            emit("act", e)

        @block.vector
        def _(e):
            emit("dve", e)

        @block.gpsimd
        def _(e):
            emit("pool", e)

        @block.sync
        def _(e):
            emit("sp", e, tail=True)

    es.close()
    return nc, tapl


_CACHE = {}


def _get_program(taps=None):
    key = tuple(sorted(taps)) if taps else ()
    if key not in _CACHE:
        _CACHE[key] = build_program(taps)
    return _CACHE[key]


def make_in_maps(x_prompt, x_sample, c_prompt, c_sample, cache_k_win, cache_v_win, state_ret,
                 w_ada, b_ada, w_in, attn_sinks, ret_gn_w, w_out, ln_w, ln_b):
    f = lambda a: np.ascontiguousarray(np.asarray(a, dtype=np.float32))
    consts = make_consts()
    shared = {
        "wada": f(w_ada[0]), "bada": f(b_ada[0]).reshape(24, 128), "win": f(w_in[0]),
        "sinks": f(attn_sinks[0]), "gnw": f(ret_gn_w[0]).reshape(4, 128), "wout": f(w_out[0]),
        "lnw": f(ln_w[0]), "lnb": f(ln_b[0]),
    }
    shared.update(consts)
    in_maps = []
    for c in range(NCORES):
        m = dict(shared)
        m["xp"] = f(x_prompt[NB * c:NB * (c + 1)])
        m["xs"] = f(x_sample[NS * c:NS * (c + 1)]).reshape(128, D)
        m["c18"] = np.concatenate([f(c_prompt[NB * c:NB * (c + 1)]), f(c_sample[NS * c:NS * (c + 1)])], axis=0)
        m["ck"] = f(cache_k_win[0, NS * c:NS * (c + 1)]).reshape(NS, 128, 128)
        m["cv"] = f(cache_v_win[0, NS * c:NS * (c + 1)]).reshape(NS, 128, 128)
        m["sr"] = f(state_ret[0, NS * c:NS * (c + 1)])
        in_maps.append(m)
    return in_maps


def kernel(**inputs):
    nc, _ = _get_program()
    in_maps = make_in_maps(**inputs)
    res = run_bass_kernel_spmd(nc, in_maps, core_ids=list(range(NCORES)))
    r = res.results
    cat = lambda k: np.concatenate([np.asarray(r[c][k]) for c in range(NCORES)], axis=0)
    y_p = cat("yp").reshape(16, 2048, D)
    y_s = cat("ys").reshape(128, 8, D)
    kwp = cat("kwp").reshape(1, 16, 128, 2, 64)
    vwp = cat("vwp").reshape(1, 16, 128, 2, 64)
    srp = cat("srp").reshape(1, 16, 4, 128, 128)
    kws = cat("kws").reshape(1, 128, 128, 2, 64)
    vws = cat("vws").reshape(1, 128, 128, 2, 64)
    srs = cat("srs").reshape(1, 128, 4, 128, 128)
    return (y_p.astype(np.float32), y_s.astype(np.float32), kwp.astype(np.float32), vwp.astype(np.float32),
            srp.astype(np.float32), kws.astype(np.float32), vws.astype(np.float32), srs.astype(np.float32))
```
